# Optimizing a Trainium2 kernel written in Bass

```python
import math
import jax, jax.numpy as jnp
from jax import lax
import numpy as np

D_MODEL = 1024
BATCH = 4
SEQ = 4096
DEPTH = 2

CHUNK = 64
Q_BLOCK = 128
SB_HEADS = 16
SB_HEAD_DIM = D_MODEL // SB_HEADS
SB_WIDTH = SB_HEADS * SB_HEAD_DIM
CONV_WIDTH = D_MODEL
CONV_K = 3
N_BRANCH = 2
FFN_HIDDEN = -(-8 * D_MODEL // (3 * 256)) * 256
IN_WIDTH = 3 * SB_WIDTH + 3 * CONV_WIDTH + N_BRANCH * D_MODEL
EPS = 1e-6

kernel_name = "stickbreak_shortconv_griffin_adaln_block"


def rmsnorm(x, g):
    xf = x.astype(jnp.float32)
    y = xf * lax.rsqrt(jnp.mean(xf * xf, axis=-1, keepdims=True) + EPS)
    return (y * g.astype(jnp.float32)).astype(x.dtype)


def stick_breaking_attention(q, k, v):
    b, s_len, h, dh = q.shape
    qh = jnp.transpose(q, (0, 2, 1, 3)).astype(jnp.float32)
    kh = jnp.transpose(k, (0, 2, 1, 3)).astype(jnp.float32)
    vh = jnp.transpose(v, (0, 2, 1, 3)).astype(jnp.float32)
    inv_sqrt = 1.0 / math.sqrt(dh)
    outs = []
    for i in range(s_len // Q_BLOCK):
        t0 = i * Q_BLOCK
        n_keys = t0 + Q_BLOCK
        qb = qh[:, :, t0:t0 + Q_BLOCK]
        kp = kh[:, :, :n_keys]
        vp = vh[:, :, :n_keys]
        z = jnp.einsum('bhqd,bhkd->bhqk', qb, kp) * inv_sqrt
        t_idx = t0 + jnp.arange(Q_BLOCK)[:, None]
        s_idx = jnp.arange(n_keys)[None, :]
        mask = s_idx < t_idx
        log_not = jnp.where(mask, jax.nn.log_sigmoid(-z), 0.0)
        excl = lax.cumsum(log_not, axis=3, reverse=True) - log_not
        log_a = jax.nn.log_sigmoid(z) + excl
        a = jnp.where(mask, jnp.exp(log_a), 0.0)
        outs.append(jnp.einsum('bhqk,bhkd->bhqd', a, vp))
    o = jnp.concatenate(outs, axis=2)
    o = jnp.transpose(o, (0, 2, 1, 3)).reshape(b, s_len, h * dh)
    return o.astype(q.dtype)


def causal_dwconv(x, w):
    ch = x.shape[-1]
    return lax.conv_general_dilated(
        x, w[:, None, :].astype(x.dtype), window_strides=(1,),
        padding=[(CONV_K - 1, 0)], dimension_numbers=('NWC', 'WIO', 'NWC'),
        feature_group_count=ch)


def setup_inputs(seed: int = 0) -> dict:
    key = jax.random.key(seed)
    ks = jax.random.split(key, 18)
    f32 = jnp.float32

    def nrm(k, shape, fan_in):
        return jax.random.normal(k, shape, f32) * (fan_in ** -0.5)

    def gain(k, shape):
        return 1.0 + 0.02 * jax.random.normal(k, shape, f32)

    return {
        "x": jax.random.normal(ks[0], (BATCH, SEQ, D_MODEL), f32),
        "c": jax.random.normal(ks[1], (BATCH, D_MODEL), f32),
        "ada_w": nrm(ks[2], (DEPTH, D_MODEL, 6 * D_MODEL), D_MODEL),
        "ada_b": 0.02 * jax.random.normal(ks[3], (DEPTH, 6 * D_MODEL), f32),
        "ln1_g": gain(ks[4], (DEPTH, D_MODEL)),
        "w_in": nrm(ks[5], (DEPTH, D_MODEL, IN_WIDTH), D_MODEL),
        "q_norm_g": gain(ks[6], (DEPTH, SB_HEAD_DIM)),
        "k_norm_g": gain(ks[7], (DEPTH, SB_HEAD_DIM)),
        "conv_w": nrm(ks[8], (DEPTH, CONV_K, CONV_WIDTH), CONV_K),
        "w_branch_a": nrm(ks[9], (DEPTH, SB_WIDTH, D_MODEL), SB_WIDTH),
        "w_branch_b": nrm(ks[10], (DEPTH, CONV_WIDTH, D_MODEL), CONV_WIDTH),
        "w_out": nrm(ks[11], (DEPTH, D_MODEL, D_MODEL), D_MODEL),
        "ln2_g": gain(ks[12], (DEPTH, D_MODEL)),
        "w_ffn_gate": nrm(ks[13], (DEPTH, D_MODEL, FFN_HIDDEN), D_MODEL),
        "w_ffn_up": nrm(ks[14], (DEPTH, D_MODEL, FFN_HIDDEN), D_MODEL),
        "w_ffn_down": nrm(ks[15], (DEPTH, FFN_HIDDEN, D_MODEL), FFN_HIDDEN),
    }


def reference(x, c, ada_w, ada_b, ln1_g, w_in, q_norm_g, k_norm_g, conv_w,
              w_branch_a, w_branch_b, w_out, ln2_g, w_ffn_gate, w_ffn_up, w_ffn_down):
    b, s_len, d = x.shape
    split_at = np.cumsum([SB_WIDTH, SB_WIDTH, SB_WIDTH,
                          CONV_WIDTH, CONV_WIDTH, CONV_WIDTH, D_MODEL])
    c_act = jax.nn.silu(c)
    for l in range(DEPTH):
        mod = c_act @ ada_w[l] + ada_b[l]
        sh1, sc1, g1, sh2, sc2, g2 = [m[:, None, :] for m in jnp.split(mod, 6, axis=-1)]

        h = rmsnorm(x, ln1_g[l]) * (1.0 + sc1) + sh1
        p = h @ w_in[l]
        q, k, v, cb, cc, cx, ga, gb = jnp.split(p, split_at, axis=-1)
        q = rmsnorm(q.reshape(b, s_len, SB_HEADS, SB_HEAD_DIM), q_norm_g[l])
        k = rmsnorm(k.reshape(b, s_len, SB_HEADS, SB_HEAD_DIM), k_norm_g[l])
        v = v.reshape(b, s_len, SB_HEADS, SB_HEAD_DIM)
        y_a = stick_breaking_attention(q, k, v)
        y_b = cb * causal_dwconv(cc * cx, conv_w[l])
        merged = (jax.nn.sigmoid(ga) * (y_a @ w_branch_a[l])
                  + jax.nn.sigmoid(gb) * (y_b @ w_branch_b[l]))
        x = x + g1 * (merged @ w_out[l])

        h = rmsnorm(x, ln2_g[l]) * (1.0 + sc2) + sh2
        f = (jax.nn.silu(h @ w_ffn_gate[l]) * (h @ w_ffn_up[l])) @ w_ffn_down[l]
        x = x + g2 * f
    return x
```

```python
import math
import numpy as np
import ml_dtypes
import concourse.bass as bass
import concourse.mybir as mybir
from concourse.bass_utils import run_bass_kernel_spmd

F32 = mybir.dt.float32
BF16 = mybir.dt.bfloat16
AF = mybir.ActivationFunctionType
ALU = mybir.AluOpType
AX = mybir.AxisListType

D = 1024
S = 4096
NB = 4
NSLOT = 16
TOK = 2048
FF = 2816
NM = 22
EPS = 1e-6
SEG = 3000
ARENA_BYTES = 212000
WORK_BASE = 104960
PAIRS = [[0, 1], [2, 3], [4, 5], [6, 7]]


class Buf:
    __slots__ = ("name", "lw", "pws", "rd", "sem", "cnt")

    def __init__(self, name):
        self.name = name
        self.lw = None
        self.pws = []
        self.rd = []
        self.sem = None
        self.cnt = 0


class Tile:
    def __init__(self, ap, buf):
        self.ap = ap
        self.buf = buf

    def __getitem__(self, k):
        return self.ap[k]


class Op:
    __slots__ = ("eng", "fn", "deps", "signal", "sidx", "is_dma", "sem", "val", "nobar", "rawdeps")

    def __init__(self, eng, fn):
        self.eng = eng
        self.fn = fn
        self.deps = []
        self.rawdeps = set()
        self.signal = False
        self.sidx = None
        self.is_dma = False
        self.sem = None
        self.val = 0
        self.nobar = False


class Sched:
    def __init__(self, nc):
        self.nc = nc
        self.engs = {"pe": nc.tensor, "act": nc.scalar, "dve": nc.vector, "pool": nc.gpsimd, "sp": nc.sync}
        self.ops = {k: [] for k in self.engs}
        self.since_bar = []
        self.bar_deps = {k: None for k in self.engs}
        self.sem_pool = []
        self.sem_pool_i = 0
        self.fixed_sems = {}

    def _bufs(self, lst):
        return [t.buf if isinstance(t, Tile) else t for t in lst]

    def _record(self, o, r, w, pw):
        deps = []
        raw = set()
        for b in r:
            if b.lw is not None:
                deps.append(b.lw)
                raw.add(id(b.lw))
            for p in b.pws:
                deps.append(p)
                raw.add(id(p))
        for b in pw:
            if b.lw is not None:
                deps.append(b.lw)
            deps.extend(b.rd)
        for b in w:
            if b.lw is not None:
                deps.append(b.lw)
            deps.extend(b.pws)
            deps.extend(b.rd)
        bd = self.bar_deps[o.eng]
        if bd is not None:
            deps.extend(bd)
            for d in bd:
                raw.add(id(d))
            self.bar_deps[o.eng] = None
        seen = set()
        out = []
        for d in deps:
            if d is o or id(d) in seen:
                continue
            seen.add(id(d))
            out.append(d)
        o.deps = out
        o.rawdeps = raw
        for b in r:
            b.rd.append(o)
        for b in pw:
            b.pws.append(o)
        for b in w:
            b.lw = o
            b.pws = []
            b.rd = []
        self.ops[o.eng].append(o)
        if not o.nobar:
            self.since_bar.append(o)

    def op(self, eng, fn, r=(), w=(), pw=()):
        o = Op(eng, fn)
        self._record(o, self._bufs(r), self._bufs(w), self._bufs(pw))
        return o

    def get_sem(self, buf):
        if buf.sem is None:
            if self.sem_pool_i >= len(self.sem_pool):
                self.sem_pool.append([self.nc.alloc_semaphore(name=f"dq{len(self.sem_pool)}"), 0])
            buf.sem = self.sem_pool[self.sem_pool_i]
            self.sem_pool_i += 1
        return buf.sem

    def dma(self, q, outs_ins, owner, r=(), w=(), pw=(), nobar=False, fixed=False):
        ob = owner.buf if isinstance(owner, Tile) else owner
        if fixed:
            if ob.sem is None:
                ob.sem = [self.nc.alloc_semaphore(name=f"fx{len(self.fixed_sems)}"), 0]
                self.fixed_sems[id(ob)] = ob.sem
            semrec = ob.sem
        else:
            semrec = self.get_sem(ob)
        n = len(outs_ins)
        semrec[1] += 16 * n
        val = semrec[1]
        sem = semrec[0]

        def fn(e, outs_ins=outs_ins, sem=sem):
            for (o_, i_) in outs_ins:
                e.dma_start(out=o_, in_=i_).then_inc(sem, 16)
            return None
        o = Op(q, fn)
        o.is_dma = True
        o.sem = sem
        o.val = val
        o.nobar = nobar
        self._record(o, self._bufs(r), self._bufs(w), self._bufs(pw))
        return o

    def collective(self, ins_t, outs_t, r, w):
        sem = self.nc.alloc_semaphore(name=f"cc{len(self.fixed_sems)}")
        self.fixed_sems[id(sem)] = sem

        def fn(e, sem=sem):
            e.collective_compute("AllGather", ALU.bypass, replica_groups=PAIRS,
                                 ins=[ins_t.ap().opt()], outs=[outs_t.ap().opt()]).then_inc(sem)
            return None
        o = Op("pool", fn)
        o.is_dma = True
        o.sem = sem
        o.val = 1
        self._record(o, self._bufs(r), self._bufs(w), [])
        return o

    def barrier(self):
        deps = list(self.since_bar)
        last = {}
        lastd = {}
        for o in deps:
            if o.is_dma:
                k = id(o.sem)
                if k not in lastd or lastd[k].val < o.val:
                    lastd[k] = o
            else:
                last[o.eng] = o
        keep = list(lastd.values())
        keep.extend(last.values())
        for k in self.engs:
            self.bar_deps[k] = list(keep)
        self.since_bar = []
        self.sem_pool_i = 0

    def emit(self):
        nc = self.nc
        for e, lst in self.ops.items():
            for o in lst:
                for d in o.deps:
                    if not d.is_dma:
                        d.signal = True
        csems = {}
        for e, lst in self.ops.items():
            n = 0
            for o in lst:
                if (not o.is_dma) and o.signal:
                    o.sidx = n
                    n += 1
            csems[e] = [nc.alloc_semaphore(name=f"c_{e}_{i}") for i in range((n + SEG - 1) // SEG + 1)]

        def run(ename, eng):
            known_c = {k: -1 for k in self.engs}
            known_d = {}
            for o in self.ops[ename]:
                for d in o.deps:
                    if d.is_dma:
                        k = id(d.sem)
                        if known_d.get(k, 0) >= d.val:
                            continue
                        known_d[k] = d.val
                        eng.wait_ge(d.sem, d.val)
                    else:
                        if d.eng == ename:
                            if ename == "pe":
                                continue
                            if id(d) not in o.rawdeps:
                                continue
                        if known_c[d.eng] >= d.sidx:
                            continue
                        known_c[d.eng] = d.sidx
                        eng.wait_ge(csems[d.eng][d.sidx // SEG], (d.sidx % SEG) + 1)
                ins = o.fn(eng)
                if (not o.is_dma) and o.signal:
                    assert ins is not None, "signalling op must return its last instruction"
                    ins.then_inc(csems[ename][o.sidx // SEG], 1)

        with nc.Block() as block:
            @block.tensor
            def _(e):
                run("pe", e)

            @block.scalar
            def _(e):
                run("act", e)

            @block.vector
            def _(e):
                run("dve", e)

            @block.gpsimd
            def _(e):
                run("pool", e)

            @block.sync
            def _(e):
                run("sp", e)


def build(layers=(0, 1), stop=None):
    nc = bass.Bass("TRN2", target_bir_lowering=False)
    S_ = Sched(nc)

    def dram_in(name, shape, dt=F32):
        return nc.dram_tensor(name, list(shape), dt, kind="ExternalInput")

    x_d = dram_in("x", [TOK, D])
    xh_d = dram_in("xh", [32, D])
    hmask_d = dram_in("hmask", [128, 32])
    hsel_d = dram_in("hsel", [32, 2])
    ccol_d = dram_in("ccol", [128, 8])
    adaw_d = dram_in("ada_w", [2, D, 6 * D])
    adab_d = dram_in("ada_b", [2, 128, 48])
    ln1_d = dram_in("ln1", [2, 128, 8])
    ln2_d = dram_in("ln2", [2, 128, 8])
    win_d = dram_in("w_in", [2, D, 8 * D])
    qg_d = dram_in("qg", [2, 128, 1])
    kg_d = dram_in("kg", [2, 128, 1])
    cw_d = dram_in("cw", [2, 128, 24])
    wa_d = dram_in("w_a", [2, D, D])
    wb_d = dram_in("w_b", [2, D, D])
    wo_d = dram_in("w_o", [2, D, D])
    wg_d = dram_in("w_g", [2, D, FF])
    wu_d = dram_in("w_u", [2, D, FF])
    wd_d = dram_in("w_d", [2, FF, D])
    maskA_d = dram_in("maskA", [128, 128], BF16)
    maskB_d = dram_in("maskB", [128, 128], BF16)
    ident_d = dram_in("ident", [128, 128], BF16)
    ntri_d = dram_in("ntri", [128, 128], BF16)
    nones_d = dram_in("nones", [128, 128], BF16)
    blk_d = dram_in("blk", [128, 128], BF16)
    identf_d = dram_in("identf", [128, 128])
    onesf_d = dram_in("onesf", [128, 128])
    out_d = nc.dram_tensor("out", [TOK, D], F32, kind="ExternalOutput")

    kt_own = [nc.dram_tensor(f"kt_own{g}", [512, TOK], BF16) for g in range(2)]
    kt_all = [nc.dram_tensor(f"kt_all{g}", [1024, TOK], BF16) for g in range(2)]
    v_own = [nc.dram_tensor(f"v_own{g}", [1024, D], BF16) for g in range(2)]
    v_all = [nc.dram_tensor(f"v_all{g}", [2048, D], BF16) for g in range(2)]
    qt_s = nc.dram_tensor("qt_s", [8, 128, TOK], BF16)
    ya_s = nc.dram_tensor("ya_s", [8, 128, TOK], BF16)
    yb_s = nc.dram_tensor("yb_s", [8, 128, TOK], BF16)
    wc_s = [nc.dram_tensor(f"wc_s{l}", [8, 128, 4, 8, 128], BF16) for l in range(2)]
    wf_s = [nc.dram_tensor(f"wf_s{l}", [NM, 128, 2, 8, 128], BF16) for l in range(2)]
    xh_own = nc.dram_tensor("xh_own", [32, D], F32)
    xh_all = nc.dram_tensor("xh_all", [64, D], F32)
    B_kt_own = [Buf(f"kt_own{g}") for g in range(2)]
    B_kt_all = [Buf(f"kt_all{g}") for g in range(2)]
    B_v_own = [Buf(f"v_own{g}") for g in range(2)]
    B_v_all = [Buf(f"v_all{g}") for g in range(2)]
    B_qt = [Buf(f"qt{i}") for i in range(8)]
    B_ya = [Buf(f"ya{i}") for i in range(8)]
    B_yb = [Buf(f"yb{i}") for i in range(8)]
    B_wc = [Buf(f"wc{l}") for l in range(2)]
    B_wf = [Buf(f"wf{l}") for l in range(2)]
    B_xho = Buf("xh_own")
    B_xha = Buf("xh_all")
    B_out = Buf("out")
    cring = [Buf("cr0"), Buf("cr1"), Buf("cr2")]
    cri = [0]

    arena = nc.alloc_sbuf_tensor("arena", [128, ARENA_BYTES // 2], BF16)

    def carve(off, shape, dt, name):
        assert off % 4 == 0
        n = 1
        for s_ in shape[1:]:
            n *= s_
        nb = n * (2 if dt == BF16 else 4)
        assert off + nb <= ARENA_BYTES, (name, off, nb)
        ap = arena[:, off // 2:(off + nb) // 2]
        if dt != BF16:
            ap = ap.bitcast(dt)
        if len(shape) == 3:
            ap = ap.rearrange("p (a b) -> p a b", a=shape[1])
        elif len(shape) == 4:
            ap = ap.rearrange("p (a b c) -> p a b c", a=shape[1], b=shape[2])
        if shape[0] < 128:
            ap = ap[0:shape[0]]
        return ap, nb

    class Alloc:
        def __init__(self, base):
            self.base = base
            self.p = base

        def new(self, name, shape, dt):
            ap, nb = carve(self.p, shape, dt, name)
            self.p += (nb + 31) // 32 * 32
            return Tile(ap, Buf(name))

        def reset(self):
            self.p = self.base

    fx = Alloc(0)
    x_sb = fx.new("x", [128, NSLOT, D], F32)
    Bx = [Buf(f"x{j}") for j in range(NSLOT)]
    hT = fx.new("hT", [128, 8, TOK + 32], BF16)
    assert fx.p <= 98816 + 64
    fx.p = 98816
    ident = fx.new("ident", [128, 128], BF16)
    ntri = fx.new("ntri", [128, 128], BF16)
    nones = fx.new("nones", [128, 128], BF16)
    blk = fx.new("blk", [128, 128], BF16)
    maskA = fx.new("maskA", [128, 128], BF16)
    maskB = fx.new("maskB", [128, 128], BF16)
    identf = fx.new("identf", [128, 128], F32)
    onesf = fx.new("onesf", [128, 128], F32)
    modc = [fx.new(f"modc{l}", [128, 48], F32) for l in range(2)]
    adab = [fx.new(f"adab{l}", [128, 48], F32) for l in range(2)]
    ln1 = [fx.new(f"ln1{l}", [128, 8], F32) for l in range(2)]
    ln2 = [fx.new(f"ln2{l}", [128, 8], F32) for l in range(2)]
    Amod = [[fx.new(f"A{l}{k}", [128, 8], F32) for k in range(2)] for l in range(2)]
    qg = [fx.new(f"qg{l}", [128, 1], F32) for l in range(2)]
    kg = [fx.new(f"kg{l}", [128, 1], F32) for l in range(2)]
    cw = [fx.new(f"cw{l}", [128, 24], F32) for l in range(2)]
    hmask = fx.new("hmask", [128, 32], F32)
    hsel = fx.new("hsel", [32, 2], F32)
    ccol = fx.new("ccol", [128, 8], F32)
    cact = fx.new("cact", [128, 8], BF16)
    stat = [fx.new(f"stat{i}", [128, 4], F32) for i in range(4)]
    assert fx.p <= WORK_BASE, fx.p
    wk = Alloc(WORK_BASE)

    pall = nc.alloc_psum_tensor("pall", [128, 4096], F32).ap()
    ps = [Tile(pall[:, i * 512:(i + 1) * 512], Buf(f"ps{i}")) for i in range(6)]
    psb = [Tile(pall[:, (6 + i) * 512:(7 + i) * 512].bitcast(BF16), Buf(f"psb{i}")) for i in range(2)]
    psctr = [0]

    def nextps():
        t = ps[psctr[0] % 6]
        psctr[0] += 1
        return t

    def ld(q, dst, src_ap, nobar=False):
        return S_.dma(q, [(dst.ap if isinstance(dst, Tile) else dst, src_ap)], dst, w=[dst], nobar=nobar)

    for t_, d_ in ((ident, ident_d), (ntri, ntri_d), (nones, nones_d), (blk, blk_d), (maskA, maskA_d),
                   (maskB, maskB_d), (identf, identf_d), (onesf, onesf_d), (hmask, hmask_d), (hsel, hsel_d),
                   (ccol, ccol_d)):
        S_.dma("sp", [(t_.ap, d_.ap())], t_, w=[t_], fixed=True)
    for l in range(2):
        for t_, d_ in ((adab[l], adab_d), (ln1[l], ln1_d), (ln2[l], ln2_d), (qg[l], qg_d), (kg[l], kg_d),
                       (cw[l], cw_d)):
            S_.dma("sp", [(t_.ap, d_.ap()[l])], t_, w=[t_], fixed=True)
    xv = x_d.ap().rearrange("(j p) f -> p j f", p=128)
    for q4 in range(4):
        S_.dma("sp", [(x_sb.ap[:, q4 * 4:(q4 + 1) * 4, :], xv[:, q4 * 4:(q4 + 1) * 4, :])], Bx[q4 * 4],
               w=[Bx[j] for j in range(q4 * 4, q4 * 4 + 4)], fixed=True)

    S_.op("act", lambda e: e.activation(out=cact.ap, in_=ccol.ap, func=AF.Silu), r=[ccol], w=[cact])
    def p0(l, wsec, pm):
        si = 0
        for sec in range(12):
            wt = wsec[si % 3]
            si += 1
            ld("pool", wt, adaw_d.ap()[l][:, sec * 512:(sec + 1) * 512].rearrange("(c p) n -> p c n", p=128))

            def fn(e, wt=wt, pm=pm, sec=sec):
                ins = None
                for fc in range(4):
                    c = sec * 4 + fc
                    for kc in range(8):
                        ins = e.matmul(pm.ap[:, c:c + 1], wt.ap[:, kc, fc * 128:(fc + 1) * 128], cact.ap[:, kc:kc + 1],
                                       start=(kc == 0), stop=(kc == 7), skip_group_check=True)
                return ins
            S_.op("pe", fn, r=[wt, cact], w=[pm])
            yield
        S_.op("dve", lambda e, l=l, pm=pm: e.tensor_tensor(out=modc[l].ap, in0=pm.ap[:, 0:48], in1=adab[l].ap, op=ALU.add),
              r=[pm, adab[l]], w=[modc[l]])
        for k, (c0, lnt) in enumerate(((8, ln1[l]), (32, ln2[l]))):
            S_.op("dve", lambda e, l=l, k=k, c0=c0, lnt=lnt: e.scalar_tensor_tensor(
                out=Amod[l][k].ap, in0=modc[l].ap[:, c0:c0 + 8], scalar=1.0, in1=lnt.ap, op0=ALU.add, op1=ALU.mult),
                r=[modc[l], lnt], w=[Amod[l][k]])
        yield

    for _ in p0(layers[0], [wk.new(f"wsec{i}", [128, 8, 512], BF16) for i in range(3)], nextps()):
        pass

    def norm_block(l, which, src_ap, src_buf, npart, dst_T, dst_cols, tmp):
        sq, xn, st, pb = tmp
        A = Amod[l][which]
        Bc0 = 0 if which == 0 else 24
        S_.op("act", lambda e: e.activation(out=sq.ap[0:npart], in_=src_ap, func=AF.Square), r=[src_buf], w=[sq])
        S_.op("dve", lambda e: e.tensor_reduce(out=st.ap[0:npart, 0:1], in_=sq.ap[0:npart], axis=AX.X, op=ALU.add),
              r=[sq], w=[st])
        S_.op("act", lambda e: e.activation(out=st.ap[0:npart, 1:2], in_=st.ap[0:npart, 0:1], func=AF.Ln,
                                            scale=1.0 / D, bias=EPS), r=[st], w=[st])
        S_.op("act", lambda e: e.activation(out=st.ap[0:npart, 2:3], in_=st.ap[0:npart, 1:2], func=AF.Exp, scale=-0.5),
              r=[st], w=[st])
        S_.op("dve", lambda e: e.tensor_scalar(out=xn.ap[0:npart], in0=src_ap, scalar1=st.ap[0:npart, 2:3], scalar2=None,
                                               op0=ALU.mult), r=[src_buf, st], w=[xn])

        def tr(e):
            ins = None
            for c in range(8):
                ins = e.transpose(pb.ap[:, c * 128:c * 128 + npart], xn.ap[0:npart, c * 128:(c + 1) * 128],
                                  ident.ap[0:npart, 0:npart])
            return ins
        S_.op("pe", tr, r=[xn, ident], w=[pb])

        def ev(e):
            ins = None
            for c in range(8):
                ins = e.tensor_scalar(out=dst_T.ap[:, c, dst_cols], in0=pb.ap[:, c * 128:c * 128 + npart],
                                      scalar1=A.ap[:, c:c + 1], scalar2=modc[l].ap[:, Bc0 + c:Bc0 + c + 1],
                                      op0=ALU.mult, op1=ALU.add)
            return ins
        S_.op("dve", ev, r=[pb, A, modc[l]], pw=[dst_T])

    def gbcast(l, c0, dst):
        for half in range(2):
            pg = nextps()
            dg = [wk.new(f"dg{half}{i}", [128, 128], F32) for i in range(4)]
            for c4 in range(4):
                c = c0 + half * 4 + c4
                S_.op("dve", lambda e, c=c, t=dg[c4]: e.tensor_scalar(out=t.ap, in0=identf.ap, scalar1=modc[l].ap[:, c:c + 1],
                                                                    scalar2=None, op0=ALU.mult), r=[identf, modc[l]], w=[dg[c4]])

            def fn(e, pg=pg, dg=dg):
                ins = None
                for c4 in range(4):
                    ins = e.matmul(pg.ap[:, c4 * 128:(c4 + 1) * 128], onesf.ap, dg[c4].ap, start=True, stop=True,
                                   skip_group_check=True)
                return ins
            S_.op("pe", fn, r=dg + [onesf], w=[pg])
            S_.op("act", lambda e, pg=pg, half=half: e.activation(out=dst.ap[:, half * 512:(half + 1) * 512], in_=pg.ap,
                                                                 func=AF.Identity), r=[pg], pw=[dst])

    def conv_gen():
        for l2 in layers:
            for m in range(8):
                prs = []
                for w_, src in enumerate((win_d.ap()[l2][:, 6 * D + m * 128:6 * D + (m + 1) * 128],
                                          win_d.ap()[l2][:, 7 * D + m * 128:7 * D + (m + 1) * 128],
                                          wa_d.ap()[l2][:, m * 128:(m + 1) * 128],
                                          wb_d.ap()[l2][:, m * 128:(m + 1) * 128])):
                    prs.append((wc_s[l2].ap()[m, :, w_, :, :], src.rearrange("(c p) n -> p c n", p=128)))
                S_.dma("pool", prs, cring[cri[0] % 3], pw=[B_wc[l2]], w=[cring[cri[0] % 3]], nobar=True, fixed=True)
                cri[0] += 1
                yield
            for m in range(NM):
                prs = []
                for w_, src in enumerate((wg_d.ap()[l2][:, m * 128:(m + 1) * 128], wu_d.ap()[l2][:, m * 128:(m + 1) * 128])):
                    prs.append((wf_s[l2].ap()[m, :, w_, :, :], src.rearrange("(c p) n -> p c n", p=128)))
                S_.dma("pool", prs, cring[cri[0] % 3], pw=[B_wf[l2]], w=[cring[cri[0] % 3]], nobar=True, fixed=True)
                cri[0] += 1
                yield


    def layer_body(li, l):
        last_layer = (li == len(layers) - 1)
        if stop == (li, 'P0'):
            return True
        S_.barrier()
        wk.reset()
        xh = wk.new("xh", [32, D], F32)
        if li == 0:
            ld("sp", xh, xh_d.ap())
        else:
            c0t = wk.new("c0t", [32, D], F32)
            c1t = wk.new("c1t", [32, D], F32)
            S_.dma("sp", [(xh_own.ap().rearrange("(j i) f -> i j f", i=2), x_sb.ap[126:128, :, :])], B_xho, r=Bx, w=[B_xho],
                   fixed=True)
            S_.collective(xh_own, xh_all, r=[B_xho], w=[B_xha])
            S_.dma("sp", [(c0t.ap, xh_all.ap()[0:32, :])], c0t, r=[B_xha], w=[c0t])
            S_.dma("sp", [(c1t.ap[2:32], xh_all.ap()[32:62, :]), (c1t.ap[0:2], xh_all.ap()[32:34, :])], c1t, r=[B_xha], w=[c1t])
            S_.op("dve", lambda e: e.tensor_scalar(out=c0t.ap, in0=c0t.ap, scalar1=hsel.ap[:, 0:1], scalar2=None, op0=ALU.mult),
                  r=[c0t, hsel], w=[c0t])
            S_.op("dve", lambda e: e.scalar_tensor_tensor(out=xh.ap, in0=c1t.ap, scalar=hsel.ap[:, 1:2], in1=c0t.ap,
                                                          op0=ALU.mult, op1=ALU.add), r=[c0t, c1t, hsel], w=[xh])
        tmps = [(wk.new(f"sq{i}", [128, D], BF16), wk.new(f"xn{i}", [128, D], BF16), stat[i], psb[i]) for i in range(2)]
        for j in range(NSLOT):
            norm_block(l, 0, x_sb.ap[:, j, :], Bx[j], 128, hT, slice(j * 128, (j + 1) * 128), tmps[j % 2])
        norm_block(l, 0, xh.ap, xh.buf, 32, hT, slice(TOK, TOK + 32), tmps[0])

        if stop == (li, 'A1'):
            return True
        S_.barrier()
        wk.reset()
        wsec = [wk.new(f"wsec{i}", [128, 8, 512], BF16) for i in range(3)]
        sqt = [wk.new(f"sqt{i}", [128, 512], BF16) for i in range(2)]
        lnv = [wk.new(f"lnv{i}", [128, 512], F32) for i in range(2)]
        rst = [wk.new(f"rst{i}", [128, 512], F32) for i in range(2)]
        ktile = [wk.new(f"ktile{i}", [128, TOK], BF16) for i in range(2)]
        vt = [wk.new(f"vt{i}", [128, 512], BF16) for i in range(3)]
        si = 0
        cnt = 0
        kti = 0
        vti = 0
        pending = [None]
        for kind, base in (("k", D), ("v", 2 * D), ("q", 0)):
            for s in range(2):
                wt = wsec[si % 3]
                si += 1
                ld("pool", wt, win_d.ap()[l][:, base + s * 512: base + (s + 1) * 512].rearrange("(c p) n -> p c n", p=128))
                if kind in ("k", "q"):
                    gcol = kg[l] if kind == "k" else qg[l]
                    ebias = 0.0 if kind == "k" else math.log(0.125)
                    for mm in range(4):
                        m = s * 4 + mm
                        kt_ = ktile[kti % 2]
                        kti += 1
                        for tc in range(4):
                            p1 = nextps()
                            i2 = cnt % 2
                            cnt += 1

                            def f1(e, p1=p1, wt=wt, mm=mm, tc=tc):
                                ins = None
                                for kc in range(8):
                                    ins = e.matmul(p1.ap, wt.ap[:, kc, mm * 128:(mm + 1) * 128], hT.ap[:, kc, tc * 512:(tc + 1) * 512],
                                                   start=(kc == 0), stop=(kc == 7))
                                return ins
                            S_.op("pe", f1, r=[wt, hT], w=[p1])
                            S_.op("act", lambda e, p1=p1, i2=i2: e.activation(out=sqt[i2].ap, in_=p1.ap, func=AF.Square),
                                  r=[p1], w=[sqt[i2]])
                            if pending[0] is not None:
                                pending[0]()

                            def back(p1=p1, i2=i2, kt_=kt_, tc=tc, gcol=gcol, ebias=ebias, kind=kind, s=s, mm=mm, m=m):
                                p2 = nextps()
                                S_.op("pe", lambda e: e.matmul(p2.ap, blk.ap, sqt[i2].ap, start=True, stop=True),
                                      r=[blk, sqt[i2]], w=[p2])
                                S_.op("act", lambda e: e.activation(out=lnv[i2].ap, in_=p2.ap, func=AF.Ln, scale=1.0 / 64, bias=EPS),
                                      r=[p2], w=[lnv[i2]])
                                S_.op("act", lambda e: e.activation(out=rst[i2].ap, in_=lnv[i2].ap, func=AF.Exp, scale=-0.5, bias=ebias),
                                      r=[lnv[i2]], w=[rst[i2]])
                                S_.op("dve", lambda e: e.scalar_tensor_tensor(
                                    out=kt_.ap[:, tc * 512:(tc + 1) * 512], in0=p1.ap, scalar=gcol.ap[:, 0:1], in1=rst[i2].ap,
                                    op0=ALU.mult, op1=ALU.mult), r=[p1, rst[i2], gcol], pw=[kt_])
                                if tc == 3:
                                    if kind == "k":
                                        S_.dma("sp", [(kt_own[s].ap()[mm * 128:(mm + 1) * 128, :], kt_.ap)], kt_, r=[kt_], pw=[B_kt_own[s]])
                                    else:
                                        S_.dma("sp", [(qt_s.ap()[m], kt_.ap)], kt_, r=[kt_], w=[B_qt[m]])
                            pending[0] = back
                    if pending[0] is not None:
                        pending[0]()
                        pending[0] = None
                    if kind == "k":
                        S_.collective(kt_own[s], kt_all[s], r=[B_kt_own[s]], w=[B_kt_all[s]])
                else:
                    for tb in range(NSLOT):
                        p1 = nextps()
                        v_ = vt[vti % 3]
                        vti += 1

                        def f1(e, p1=p1, wt=wt, tb=tb):
                            ins = None
                            for kc in range(8):
                                ins = e.matmul(p1.ap, hT.ap[:, kc, tb * 128:(tb + 1) * 128], wt.ap[:, kc, :],
                                               start=(kc == 0), stop=(kc == 7))
                            return ins
                        S_.op("pe", f1, r=[wt, hT], w=[p1])
                        S_.op("act", lambda e, p1=p1, v_=v_: e.activation(out=v_.ap, in_=p1.ap, func=AF.Identity), r=[p1], w=[v_])
                        g = tb // 8
                        S_.dma("sp", [(v_own[g].ap()[(tb % 8) * 128:(tb % 8 + 1) * 128, s * 512:(s + 1) * 512], v_.ap)], v_,
                               r=[v_], pw=[B_v_own[g]])
                    if s == 1:
                        for g in range(2):
                            S_.collective(v_own[g], v_all[g], r=[B_v_own[g]], w=[B_v_all[g]])

        if stop == (li, 'A2a'):
            return True
        S_.barrier()
        wk.reset()
        wconv = [wk.new(f"wconv{i}", [128, 8, 3, 512], BF16) for i in range(2)]
        ccs = [wk.new(f"ccs{i}", [128, 512], F32) for i in range(2)]
        ut = wk.new("ut", [128, TOK + 32], F32)
        cv = wk.new("cv", [128, TOK], F32)
        ybt = [wk.new(f"ybt{i}", [128, TOK], BF16) for i in range(2)]
        for s in range(2):
            S_.dma("pool", [(wconv[s].ap[:, :, j, :],
                             win_d.ap()[l][:, (3 + j) * D + s * 512:(3 + j) * D + (s + 1) * 512].rearrange("(c p) n -> p c n", p=128))
                            for j in range(3)], wconv[s], w=[wconv[s]])
        cwl = cw[l]
        for m in range(8):
            s = m // 4
            mm = m % 4
            wv = wconv[s]
            for tc in range(5):
                cols = slice(tc * 512, (tc + 1) * 512) if tc < 4 else slice(TOK, TOK + 32)
                n = 512 if tc < 4 else 32
                pc = nextps()
                px = nextps()
                cc_ = ccs[tc % 2]

                def fcc(e, pc=pc, wv=wv, mm=mm, cols=cols, n=n, j=1):
                    ins = None
                    for kc in range(8):
                        ins = e.matmul(pc.ap[:, 0:n], wv.ap[:, kc, j, mm * 128:(mm + 1) * 128], hT.ap[:, kc, cols],
                                       start=(kc == 0), stop=(kc == 7))
                    return ins
                S_.op("pe", fcc, r=[wv, hT], w=[pc])
                S_.op("act", lambda e, pc=pc, cc_=cc_, n=n: e.activation(out=cc_.ap[:, 0:n], in_=pc.ap[:, 0:n], func=AF.Identity),
                      r=[pc], w=[cc_])
                S_.op("pe", lambda e, px=px, wv=wv, mm=mm, cols=cols, n=n: fcc(e, px, wv, mm, cols, n, 2), r=[wv, hT], w=[px])
                S_.op("dve", lambda e, px=px, cc_=cc_, cols=cols, n=n: e.tensor_tensor(out=ut.ap[:, cols], in0=cc_.ap[:, 0:n],
                                                                                     in1=px.ap[:, 0:n], op=ALU.mult),
                      r=[px, cc_], pw=[ut])
            S_.op("dve", lambda e: e.tensor_tensor(out=ut.ap[:, TOK:TOK + 32], in0=ut.ap[:, TOK:TOK + 32], in1=hmask.ap, op=ALU.mult),
                  r=[ut, hmask], w=[ut])
            w0 = cwl.ap[:, 0 * 8 + m:0 * 8 + m + 1]
            w1 = cwl.ap[:, 1 * 8 + m:1 * 8 + m + 1]
            w2 = cwl.ap[:, 2 * 8 + m:2 * 8 + m + 1]
            u3 = ut.ap[:, 0:TOK].rearrange("p (j t) -> p j t", t=128)
            c3 = cv.ap.rearrange("p (j t) -> p j t", t=128)
            uh = ut.ap[:, TOK:TOK + 32].rearrange("p (j i) -> p j i", i=2)
            S_.op("dve", lambda e, w2=w2: e.tensor_scalar(out=cv.ap, in0=ut.ap[:, 0:TOK], scalar1=w2, scalar2=None, op0=ALU.mult),
                  r=[ut, cwl], w=[cv])
            S_.op("dve", lambda e, w1=w1, u3=u3, c3=c3: e.scalar_tensor_tensor(out=c3[:, :, 1:128], in0=u3[:, :, 0:127], scalar=w1,
                                                                            in1=c3[:, :, 1:128], op0=ALU.mult, op1=ALU.add),
                  r=[ut, cv, cwl], w=[cv])
            S_.op("dve", lambda e, w0=w0, u3=u3, c3=c3: e.scalar_tensor_tensor(out=c3[:, :, 2:128], in0=u3[:, :, 0:126], scalar=w0,
                                                                            in1=c3[:, :, 2:128], op0=ALU.mult, op1=ALU.add),
                  r=[ut, cv, cwl], w=[cv])
            for (ci, hi, wsc) in ((0, 1, w1), (0, 0, w0), (1, 1, w0)):
                S_.op("dve", lambda e, ci=ci, hi=hi, wsc=wsc, uh=uh, c3=c3: e.scalar_tensor_tensor(
                    out=c3[:, :, ci], in0=uh[:, :, hi], scalar=wsc, in1=c3[:, :, ci], op0=ALU.mult, op1=ALU.add),
                    r=[ut, cv, cwl], w=[cv])
            yb_ = ybt[m % 2]
            for tc in range(4):
                pb_ = nextps()
                cols = slice(tc * 512, (tc + 1) * 512)
                S_.op("pe", lambda e, pb_=pb_, wv=wv, mm=mm, cols=cols: fcc(e, pb_, wv, mm, cols, 512, 0), r=[wv, hT], w=[pb_])
                S_.op("dve", lambda e, pb_=pb_, yb_=yb_, cols=cols: e.tensor_tensor(out=yb_.ap[:, cols], in0=cv.ap[:, cols],
                                                                                  in1=pb_.ap, op=ALU.mult), r=[pb_, cv], pw=[yb_])
            S_.dma("sp", [(yb_s.ap()[m], yb_.ap)], yb_, r=[yb_], w=[B_yb[m]])

        if stop == (li, 'A2b'):
            return True
        S_.barrier()
        wk.reset()
        KT = [wk.new(f"KT{i}", [128, S], BF16) for i in range(2)]
        Vt = [wk.new(f"V{i}", [128, 32, 128], BF16) for i in range(2)]
        QT = [wk.new(f"QT{i}", [128, TOK], BF16) for i in range(2)]
        NR = 3
        et = [wk.new(f"e{i}", [128, 2, 512], F32) for i in range(NR)]
        spt = [wk.new(f"sp{i}", [128, 2, 512], BF16) for i in range(NR)]
        at = [wk.new(f"a{i}", [128, 2, 512], BF16) for i in range(NR)]
        sacc = wk.new("sacc", [128, 2, 512], BF16)
        yat = [wk.new(f"yat{i}", [128, TOK], BF16) for i in range(2)]
        zb = [Tile(pall[:, k * 1024:(k + 1) * 1024].rearrange("p (h n) -> p h n", h=2), Buf(f"z2_{k}")) for k in range(3)]
        OT = Tile(pall[:, 3072:3584], Buf("OT"))

        def load_hp(hp):
            g = hp // 4
            hl = hp % 4
            kt_, v_, q_ = KT[hp % 2], Vt[hp % 2], QT[hp % 2]
            S_.dma("sp", [(kt_.ap[:, r * TOK:(r + 1) * TOK], kt_all[g].ap()[r * 512 + hl * 128:r * 512 + (hl + 1) * 128, :])
                          for r in range(2)], kt_, r=[B_kt_all[g]], w=[kt_])
            S_.dma("sp", [(v_.ap[:, r * 16 + g2 * 8:r * 16 + g2 * 8 + 8, :],
                           v_all[g2].ap()[r * 1024:(r + 1) * 1024, hp * 128:(hp + 1) * 128].rearrange("(j p) c -> p j c", p=128))
                          for g2 in range(2) for r in range(2)], v_, r=B_v_all, w=[v_])
            S_.dma("sp", [(q_.ap, qt_s.ap()[hp])], q_, r=[B_qt[hp]], w=[q_])

        load_hp(0)
        items = []
        for hp in range(8):
            for G in range(4):
                for kb in range(8 * G + 7, -1, -1):
                    items.append((hp, G, kb))
        D1, D2 = 1, 2
        nit = len(items)

        def stage1(n, part):
            hp, G, kb = items[n]
            if part == 0 and G == 0 and kb == 7 and hp + 1 < 8:
                load_hp(hp + 1)
            i0 = (kb - 8 * G) // 2 if kb >= 8 * G else 0
            c0 = i0 * 128
            kt_, q_ = KT[hp % 2], QT[hp % 2]
            z = zb[n % 3]
            e_, sp_ = et[n % NR], spt[n % NR]
            kcol = ((kb % 2) * 16 + kb // 2) * 128

            def f1(e):
                ins = None
                for hd in range(2):
                    r0 = hd * 64
                    ins = e.matmul(z.ap[:, hd, c0:512], kt_.ap[r0:r0 + 64, kcol:kcol + 128],
                                   q_.ap[r0:r0 + 64, G * 512 + c0:(G + 1) * 512], start=True, stop=True)
                return ins
            if part == 0:
                S_.op("pe", f1, r=[kt_, q_], w=[z])
                S_.op("act", lambda e: e.activation(out=e_.ap[:, :, c0:512], in_=z.ap[:, :, c0:512], func=AF.Exp), r=[z], w=[e_])
                return
            S_.op("act", lambda e: e.activation(out=sp_.ap[:, :, c0:512], in_=e_.ap[:, :, c0:512], func=AF.Ln, bias=1.0),
                  r=[e_], w=[sp_])
            if kb >= 8 * G:
                mk = maskB if (kb - 8 * G) % 2 == 1 else maskA

                def fmk(e):
                    ins = None
                    for hd in range(2):
                        ins = e.tensor_tensor(out=sp_.ap[:, hd, c0:c0 + 128], in0=sp_.ap[:, hd, c0:c0 + 128], in1=mk.ap, op=ALU.mult)
                    return ins
                S_.op("dve", fmk, r=[sp_, mk], w=[sp_])

        def stage2(n):
            hp, G, kb = items[n]
            i0 = (kb - 8 * G) // 2 if kb >= 8 * G else 0
            c0 = i0 * 128
            z = zb[n % 3]
            sp_, a_ = spt[n % NR], at[n % NR]
            first = (kb == 8 * G + 7)
            if first:
                S_.op("dve", lambda e: e.memset(sacc.ap, 0.0), w=[sacc])

            def f2(e):
                ins = None
                for hd in range(2):
                    ins = e.matmul(z.ap[:, hd, c0:512], ntri.ap, sp_.ap[:, hd, c0:512], start=False, stop=True, skip_group_check=True)
                    if not first:
                        ins = e.matmul(z.ap[:, hd, c0:512], nones.ap, sacc.ap[:, hd, c0:512], start=False, stop=True,
                                       skip_group_check=True)
                return ins
            S_.op("pe", f2, r=[ntri, nones, sp_, sacc], w=[z])
            if kb > 0:
                S_.op("dve", lambda e: e.tensor_tensor(out=sacc.ap[:, :, c0:512], in0=sacc.ap[:, :, c0:512], in1=sp_.ap[:, :, c0:512],
                                                       op=ALU.add), r=[sp_, sacc], w=[sacc])
            S_.op("act", lambda e: e.activation(out=a_.ap[:, :, c0:512], in_=z.ap[:, :, c0:512], func=AF.Exp), r=[z], w=[a_])
            if kb >= 8 * G:
                mk = maskB if (kb - 8 * G) % 2 == 1 else maskA

                def fmk(e):
                    ins = None
                    for hd in range(2):
                        ins = e.tensor_tensor(out=a_.ap[:, hd, c0:c0 + 128], in0=a_.ap[:, hd, c0:c0 + 128], in1=mk.ap, op=ALU.mult)
                    return ins
                S_.op("dve", fmk, r=[a_, mk], w=[a_])

        def stage3(n):
            hp, G, kb = items[n]
            i0 = (kb - 8 * G) // 2 if kb >= 8 * G else 0
            c0 = i0 * 128
            a_ = at[n % NR]
            v_ = Vt[hp % 2]
            vb = (kb % 2) * 16 + kb // 2
            first = (kb == 8 * G + 7)

            def f3(e):
                ins = None
                for hd in range(2):
                    ins = e.matmul(OT.ap[hd * 64:(hd + 1) * 64, c0:512], v_.ap[:, vb, hd * 64:(hd + 1) * 64], a_.ap[:, hd, c0:512],
                                   start=first, stop=(kb == 0), skip_group_check=True)
                return ins
            S_.op("pe", f3, r=[v_, a_], w=[OT])
            if kb == 0:
                y_ = yat[hp % 2]
                S_.op("act", lambda e: e.activation(out=y_.ap[:, G * 512:(G + 1) * 512], in_=OT.ap, func=AF.Identity), r=[OT], pw=[y_])
                if G == 3:
                    S_.dma("sp", [(ya_s.ap()[hp], y_.ap)], y_, r=[y_], w=[B_ya[hp]])

        cgen = conv_gen() if li == 0 else None
        pgen = None
        if li == 0 and len(layers) > 1:
            pgen = p0(layers[1], [wk.new(f"wsecp{i}", [128, 8, 512], BF16) for i in range(3)],
                      Tile(pall[:, 3584:4096], Buf("pm7")))
        for step in range(nit + D2):
            if pgen is not None and step % 10 == 2:
                if next(pgen, "done") == "done":
                    pgen = None
            elif pgen is None and cgen is not None and step % 3 == 0:
                if next(cgen, "done") == "done":
                    cgen = None
            if step < nit:
                stage1(step, 0)
            if 0 <= step - D1 < nit:
                stage2(step - D1)
            if step < nit:
                stage1(step, 1)
            if 0 <= step - D2 < nit:
                stage3(step - D2)
        if cgen is not None:
            for _ in cgen:
                pass

        if stop == (li, 'ATT'):
            return True
        S_.barrier()
        wk.reset()
        wo = wk.new("wo", [128, 8, D], BF16)
        g1b = wk.new("g1b", [128, D], F32)
        wct = [wk.new(f"wct{i}", [128, 4, 8, 128], BF16) for i in range(2)]
        yac = wk.new("yac", [128, 8, 512], BF16)
        ybc = wk.new("ybc", [128, 8, 512], BF16)
        sgm = [wk.new(f"sgm{i}", [128, 512], F32) for i in range(4)]
        tt = [wk.new(f"tt{i}", [128, 512], F32) for i in range(4)]
        mg = wk.new("mg", [128, 8, 512], BF16)
        ld("pool", wo, wo_d.ap()[l].rearrange("(c p) n -> p c n", p=128))
        gbcast(l, 16, g1b)
        wi = 0
        for tc in range(4):
            cols = slice(tc * 512, (tc + 1) * 512)
            S_.dma("sp", [(yac.ap, ya_s.ap()[:, :, cols].rearrange("m p t -> p m t"))], yac, r=B_ya, w=[yac])
            S_.dma("sp", [(ybc.ap, yb_s.ap()[:, :, cols].rearrange("m p t -> p m t"))], ybc, r=B_yb, w=[ybc])
            for m in range(8):
                wc_ = wct[wi % 2]
                wi += 1
                S_.dma("sp", [(wc_.ap, wc_s[l].ap()[m])], wc_, r=[B_wc[l]], w=[wc_])
                pp = [nextps() for _ in range(4)]
                for w_ in range(4):
                    rhs_t = hT if w_ < 2 else (yac if w_ == 2 else ybc)

                    def fm(e, p=pp[w_], wc_=wc_, w_=w_, rhs_t=rhs_t, cols=cols):
                        ins = None
                        for kc in range(8):
                            rhs = rhs_t.ap[:, kc, cols] if w_ < 2 else rhs_t.ap[:, kc, :]
                            ins = e.matmul(p.ap, wc_.ap[:, w_, kc, :], rhs, start=(kc == 0), stop=(kc == 7))
                        return ins
                    S_.op("pe", fm, r=[wc_, rhs_t], w=[pp[w_]])
                sa, sb = sgm[(2 * m) % 4], sgm[(2 * m + 1) % 4]
                t1, t2 = tt[(2 * m) % 4], tt[(2 * m + 1) % 4]
                S_.op("act", lambda e, p=pp[0], sa=sa: e.activation(out=sa.ap, in_=p.ap, func=AF.Sigmoid), r=[pp[0]], w=[sa])
                S_.op("act", lambda e, p=pp[1], sb=sb: e.activation(out=sb.ap, in_=p.ap, func=AF.Sigmoid), r=[pp[1]], w=[sb])
                S_.op("dve", lambda e, p=pp[2], sa=sa, t1=t1: e.tensor_tensor(out=t1.ap, in0=sa.ap, in1=p.ap, op=ALU.mult),
                      r=[pp[2], sa], w=[t1])
                S_.op("dve", lambda e, p=pp[3], sb=sb, t2=t2: e.tensor_tensor(out=t2.ap, in0=sb.ap, in1=p.ap, op=ALU.mult),
                      r=[pp[3], sb], w=[t2])
                S_.op("dve", lambda e, t1=t1, t2=t2, m=m: e.tensor_tensor(out=mg.ap[:, m, :], in0=t1.ap, in1=t2.ap, op=ALU.add),
                      r=[t1, t2], pw=[mg])
            if tc == 0:
                for kc in range(8):
                    S_.op("dve", lambda e, kc=kc: e.tensor_tensor(out=wo.ap[:, kc, :], in0=wo.ap[:, kc, :], in1=g1b.ap, op=ALU.mult),
                          r=[wo, g1b], w=[wo])
            for tb in range(4):
                j = tc * 4 + tb
                for ch in range(2):
                    po = nextps()

                    def fo(e, po=po, tb=tb, ch=ch):
                        ins = None
                        for m in range(8):
                            ins = e.matmul(po.ap, mg.ap[:, m, tb * 128:(tb + 1) * 128], wo.ap[:, m, ch * 512:(ch + 1) * 512],
                                           start=(m == 0), stop=(m == 7))
                        return ins
                    S_.op("pe", fo, r=[mg, wo], w=[po])
                    S_.op("dve", lambda e, po=po, j=j, ch=ch: e.tensor_tensor(out=x_sb.ap[:, j, ch * 512:(ch + 1) * 512],
                                                                            in0=x_sb.ap[:, j, ch * 512:(ch + 1) * 512], in1=po.ap, op=ALU.add),
                          r=[po, Bx[j]], w=[Bx[j]])

        if stop == (li, 'C'):
            return True
        S_.barrier()
        wk.reset()
        wd = wk.new("wd", [128, NM, D], BF16)
        g2b = wk.new("g2b", [128, D], F32)
        wft = [wk.new(f"wft{i}", [128, 2, 8, 128], BF16) for i in range(2)]
        h2c = wk.new("h2c", [128, 8, 512], BF16)
        actT = wk.new("actT", [128, NM, 512], BF16)
        sg = [wk.new(f"sg{i}", [128, 512], F32) for i in range(2)]
        tmps = [(wk.new(f"sq{i}", [128, D], BF16), wk.new(f"xn{i}", [128, D], BF16), stat[i], psb[i]) for i in range(2)]
        wdv = wd_d.ap()[l].rearrange("(m p) n -> p m n", p=128)
        for q3 in range(2):
            S_.dma("pool", [(wd.ap[:, q3 * 11:(q3 + 1) * 11, :], wdv[:, q3 * 11:(q3 + 1) * 11, :])], wd, pw=[wd])
        gbcast(l, 40, g2b)
        wi = 0
        for tc in range(4):
            for tb in range(4):
                j = tc * 4 + tb
                norm_block(l, 1, x_sb.ap[:, j, :], Bx[j], 128, h2c, slice(tb * 128, (tb + 1) * 128), tmps[tb % 2])
            for m in range(NM):
                wf_ = wft[wi % 2]
                wi += 1
                S_.dma("sp", [(wf_.ap, wf_s[l].ap()[m])], wf_, r=[B_wf[l]], w=[wf_])
                pg, pu = nextps(), nextps()
                for w_, p in ((0, pg), (1, pu)):
                    def fm(e, p=p, wf_=wf_, w_=w_):
                        ins = None
                        for kc in range(8):
                            ins = e.matmul(p.ap, wf_.ap[:, w_, kc, :], h2c.ap[:, kc, :], start=(kc == 0), stop=(kc == 7))
                        return ins
                    S_.op("pe", fm, r=[wf_, h2c], w=[p])
                sg_ = sg[m % 2]
                S_.op("act", lambda e, pg=pg, sg_=sg_: e.activation(out=sg_.ap, in_=pg.ap, func=AF.Silu), r=[pg], w=[sg_])
                S_.op("dve", lambda e, pu=pu, sg_=sg_, m=m: e.tensor_tensor(out=actT.ap[:, m, :], in0=sg_.ap, in1=pu.ap, op=ALU.mult),
                      r=[pu, sg_], pw=[actT])
            if tc == 0:
                for m in range(NM):
                    S_.op("dve", lambda e, m=m: e.tensor_tensor(out=wd.ap[:, m, :], in0=wd.ap[:, m, :], in1=g2b.ap, op=ALU.mult),
                          r=[wd, g2b], w=[wd])
            for tb in range(4):
                j = tc * 4 + tb
                for ch in range(2):
                    po = nextps()

                    def fo(e, po=po, tb=tb, ch=ch):
                        ins = None
                        for m in range(NM):
                            ins = e.matmul(po.ap, actT.ap[:, m, tb * 128:(tb + 1) * 128], wd.ap[:, m, ch * 512:(ch + 1) * 512],
                                           start=(m == 0), stop=(m == NM - 1))
                        return ins
                    S_.op("pe", fo, r=[actT, wd], w=[po])
                    S_.op("dve", lambda e, po=po, j=j, ch=ch: e.tensor_tensor(out=x_sb.ap[:, j, ch * 512:(ch + 1) * 512],
                                                                            in0=x_sb.ap[:, j, ch * 512:(ch + 1) * 512], in1=po.ap, op=ALU.add),
                          r=[po, Bx[j]], w=[Bx[j]])
                if last_layer:
                    S_.dma("sp", [(out_d.ap()[j * 128:(j + 1) * 128, :], x_sb.ap[:, j, :])], Bx[j], r=[Bx[j]], pw=[B_out], fixed=True)

        return False

    for li, l in enumerate(layers):
        if layer_body(li, l):
            break

    S_.barrier()
    S_.op("sp", lambda e: None)
    S_.emit()
    return nc


_CACHE = {}


def _consts():
    bf = ml_dtypes.bfloat16
    i = np.arange(128)
    ident = np.eye(128, dtype=np.float32)
    ntri = -(i[:, None] >= i[None, :]).astype(np.float32)
    nones = -np.ones((128, 128), np.float32)
    blkm = (i[:, None] // 64 == i[None, :] // 64).astype(np.float32)
    causal = (i[:, None] < i[None, :]).astype(np.float32)
    return dict(ident=ident.astype(bf), ntri=ntri.astype(bf), nones=nones.astype(bf), blk=blkm.astype(bf),
                identf=ident, onesf=np.ones((128, 128), np.float32)), causal


def make_in_maps(x, c, ada_w, ada_b, ln1_g, w_in, q_norm_g, k_norm_g, conv_w, w_branch_a, w_branch_b, w_out,
                 ln2_g, w_ffn_gate, w_ffn_up, w_ffn_down):
    bf = ml_dtypes.bfloat16
    f = lambda a: np.ascontiguousarray(np.asarray(a, dtype=np.float32))
    x, c = f(x), f(c)
    consts, causal = _consts()
    shared = dict(
        ada_w=f(ada_w),
        ada_b=f(np.asarray(ada_b, np.float32).reshape(2, 48, 128).transpose(0, 2, 1)),
        ln1=f(np.asarray(ln1_g, np.float32).reshape(2, 8, 128).transpose(0, 2, 1)),
        ln2=f(np.asarray(ln2_g, np.float32).reshape(2, 8, 128).transpose(0, 2, 1)),
        w_in=f(w_in),
        qg=f(np.tile(np.asarray(q_norm_g, np.float32), (1, 2)).reshape(2, 128, 1)),
        kg=f(np.tile(np.asarray(k_norm_g, np.float32), (1, 2)).reshape(2, 128, 1)),
        cw=f(np.asarray(conv_w, np.float32).reshape(2, 3, 8, 128).transpose(0, 3, 1, 2).reshape(2, 128, 24)),
        w_a=f(w_branch_a), w_b=f(w_branch_b), w_o=f(w_out), w_g=f(w_ffn_gate), w_u=f(w_ffn_up), w_d=f(w_ffn_down),
        **consts)
    zeros = np.zeros((128, 128), np.float32)
    ones = np.ones((128, 128), np.float32)
    in_maps = []
    for core in range(8):
        b, h = core // 2, core % 2
        xb = x[b].reshape(32, 128, D)
        xo = np.ascontiguousarray(xb[h::2].reshape(TOK, D))
        xh = np.zeros((32, D), np.float32)
        for j in range(NSLOT):
            t0 = (2 * j + h) * 128
            if t0 >= 2:
                xh[2 * j:2 * j + 2] = x[b, t0 - 2:t0]
        hmask = np.ones((128, 32), np.float32)
        if h == 0:
            hmask[:, 0:2] = 0.0
        hsel = np.zeros((32, 2), np.float32)
        hsel[:, 0] = 1.0 if h == 1 else 0.0
        hsel[:, 1] = 0.0 if h == 1 else 1.0
        mA = causal if h == 0 else ones
        mB = zeros if h == 0 else causal
        m = dict(shared)
        m.update(x=xo, xh=xh, hmask=hmask, hsel=hsel,
                 ccol=np.ascontiguousarray(c[b].reshape(8, 128).T),
                 maskA=mA.astype(bf), maskB=mB.astype(bf))
        in_maps.append(m)
    return in_maps


def assemble(results):
    out = np.zeros((NB, S, D), np.float32)
    for core in range(8):
        b, h = core // 2, core % 2
        o = np.asarray(results[core]["out"], dtype=np.float32).reshape(NSLOT, 128, D)
        out[b].reshape(32, 128, D)[h::2] = o
    return out


def kernel(**inputs):
    if "nc" not in _CACHE:
        _CACHE["nc"] = build((0, 1))
    nc = _CACHE["nc"]
    in_maps = make_in_maps(**inputs)
    res = run_bass_kernel_spmd(nc, in_maps, core_ids=list(range(8)))
    return assemble(res.results)
```

```python
import math
import numpy as np
import ml_dtypes
import concourse.bass as bass
import concourse.mybir as mybir
from concourse.bass_utils import run_bass_kernel_spmd

F32 = mybir.dt.float32
BF16 = mybir.dt.bfloat16
AF = mybir.ActivationFunctionType
ALU = mybir.AluOpType
AX = mybir.AxisListType

D = 1024
S = 4096
NB = 4
NSLOT = 16
TOK = 2048
FF = 2816
NM = 22
EPS = 1e-6
SEG = 3000
ARENA_BYTES = 212000
WORK_BASE = 104960
PAIRS = [[0, 1], [2, 3], [4, 5], [6, 7]]


class Buf:
    __slots__ = ("name", "lw", "pws", "rd", "sem", "cnt")

    def __init__(self, name):
        self.name = name
        self.lw = None
        self.pws = []
        self.rd = []
        self.sem = None
        self.cnt = 0


class Tile:
    def __init__(self, ap, buf):
        self.ap = ap
        self.buf = buf

    def __getitem__(self, k):
        return self.ap[k]


class Op:
    __slots__ = ("eng", "fn", "deps", "signal", "sidx", "is_dma", "sem", "val", "nobar", "rawdeps")

    def __init__(self, eng, fn):
        self.eng = eng
        self.fn = fn
        self.deps = []
        self.rawdeps = set()
        self.signal = False
        self.sidx = None
        self.is_dma = False
        self.sem = None
        self.val = 0
        self.nobar = False


class Sched:
    def __init__(self, nc):
        self.nc = nc
        self.engs = {"pe": nc.tensor, "act": nc.scalar, "dve": nc.vector, "pool": nc.gpsimd, "sp": nc.sync}
        self.ops = {k: [] for k in self.engs}
        self.since_bar = []
        self.bar_deps = {k: None for k in self.engs}
        self.sem_pool = []
        self.sem_pool_i = 0
        self.fixed_sems = {}

    def _bufs(self, lst):
        return [t.buf if isinstance(t, Tile) else t for t in lst]

    def _record(self, o, r, w, pw):
        deps = []
        raw = set()
        for b in r:
            if b.lw is not None:
                deps.append(b.lw)
                raw.add(id(b.lw))
            for p in b.pws:
                deps.append(p)
                raw.add(id(p))
        for b in pw:
            if b.lw is not None:
                deps.append(b.lw)
            deps.extend(b.rd)
        for b in w:
            if b.lw is not None:
                deps.append(b.lw)
            deps.extend(b.pws)
            deps.extend(b.rd)
        bd = self.bar_deps[o.eng]
        if bd is not None:
            deps.extend(bd)
            for d in bd:
                raw.add(id(d))
            self.bar_deps[o.eng] = None
        seen = set()
        out = []
        for d in deps:
            if d is o or id(d) in seen:
                continue
            seen.add(id(d))
            out.append(d)
        o.deps = out
        o.rawdeps = raw
        for b in r:
            b.rd.append(o)
        for b in pw:
            b.pws.append(o)
        for b in w:
            b.lw = o
            b.pws = []
            b.rd = []
        self.ops[o.eng].append(o)
        if not o.nobar:
            self.since_bar.append(o)

    def op(self, eng, fn, r=(), w=(), pw=()):
        o = Op(eng, fn)
        self._record(o, self._bufs(r), self._bufs(w), self._bufs(pw))
        return o

    def get_sem(self, buf):
        if buf.sem is None:
            if self.sem_pool_i >= len(self.sem_pool):
                self.sem_pool.append([self.nc.alloc_semaphore(name=f"dq{len(self.sem_pool)}"), 0])
            buf.sem = self.sem_pool[self.sem_pool_i]
            self.sem_pool_i += 1
        return buf.sem

    def dma(self, q, outs_ins, owner, r=(), w=(), pw=(), nobar=False, fixed=False):
        ob = owner.buf if isinstance(owner, Tile) else owner
        if fixed:
            if ob.sem is None:
                ob.sem = [self.nc.alloc_semaphore(name=f"fx{len(self.fixed_sems)}"), 0]
                self.fixed_sems[id(ob)] = ob.sem
            semrec = ob.sem
        else:
            semrec = self.get_sem(ob)
        n = len(outs_ins)
        semrec[1] += 16 * n
        val = semrec[1]
        sem = semrec[0]

        def fn(e, outs_ins=outs_ins, sem=sem):
            for (o_, i_) in outs_ins:
                e.dma_start(out=o_, in_=i_).then_inc(sem, 16)
            return None
        o = Op(q, fn)
        o.is_dma = True
        o.sem = sem
        o.val = val
        o.nobar = nobar
        self._record(o, self._bufs(r), self._bufs(w), self._bufs(pw))
        return o

    def collective(self, ins_t, outs_t, r, w):
        sem = self.nc.alloc_semaphore(name=f"cc{len(self.fixed_sems)}")
        self.fixed_sems[id(sem)] = sem

        def fn(e, sem=sem):
            e.collective_compute("AllGather", ALU.bypass, replica_groups=PAIRS,
                                 ins=[ins_t.ap().opt()], outs=[outs_t.ap().opt()]).then_inc(sem)
            return None
        o = Op("pool", fn)
        o.is_dma = True
        o.sem = sem
        o.val = 1
        self._record(o, self._bufs(r), self._bufs(w), [])
        return o

    def barrier(self):
        deps = list(self.since_bar)
        last = {}
        lastd = {}
        for o in deps:
            if o.is_dma:
                k = id(o.sem)
                if k not in lastd or lastd[k].val < o.val:
                    lastd[k] = o
            else:
                last[o.eng] = o
        keep = list(lastd.values())
        keep.extend(last.values())
        for k in self.engs:
            self.bar_deps[k] = list(keep)
        self.since_bar = []
        self.sem_pool_i = 0

    def emit(self):
        nc = self.nc
        for e, lst in self.ops.items():
            for o in lst:
                for d in o.deps:
                    if not d.is_dma:
                        d.signal = True
        csems = {}
        for e, lst in self.ops.items():
            n = 0
            for o in lst:
                if (not o.is_dma) and o.signal:
                    o.sidx = n
                    n += 1
            csems[e] = [nc.alloc_semaphore(name=f"c_{e}_{i}") for i in range((n + SEG - 1) // SEG + 1)]

        def run(ename, eng):
            known_c = {k: -1 for k in self.engs}
            known_d = {}
            for o in self.ops[ename]:
                for d in o.deps:
                    if d.is_dma:
                        k = id(d.sem)
                        if known_d.get(k, 0) >= d.val:
                            continue
                        known_d[k] = d.val
                        eng.wait_ge(d.sem, d.val)
                    else:
                        if d.eng == ename:
                            if ename == "pe":
                                continue
                            if id(d) not in o.rawdeps:
                                continue
                        if known_c[d.eng] >= d.sidx:
                            continue
                        known_c[d.eng] = d.sidx
                        eng.wait_ge(csems[d.eng][d.sidx // SEG], (d.sidx % SEG) + 1)
                ins = o.fn(eng)
                if (not o.is_dma) and o.signal:
                    assert ins is not None, "signalling op must return its last instruction"
                    ins.then_inc(csems[ename][o.sidx // SEG], 1)

        with nc.Block() as block:
            @block.tensor
            def _(e):
                run("pe", e)

            @block.scalar
            def _(e):
                run("act", e)

            @block.vector
            def _(e):
                run("dve", e)

            @block.gpsimd
            def _(e):
                run("pool", e)

            @block.sync
            def _(e):
                run("sp", e)


def build(layers=(0, 1), stop=None):
    nc = bass.Bass("TRN2", target_bir_lowering=False)
    S_ = Sched(nc)

    def dram_in(name, shape, dt=F32):
        return nc.dram_tensor(name, list(shape), dt, kind="ExternalInput")

    x_d = dram_in("x", [TOK, D])
    xh_d = dram_in("xh", [32, D])
    hmask_d = dram_in("hmask", [128, 32])
    hsel_d = dram_in("hsel", [32, 2])
    ccol_d = dram_in("ccol", [128, 8])
    adaw_d = dram_in("ada_w", [2, D, 6 * D])
    adab_d = dram_in("ada_b", [2, 128, 48])
    ln1_d = dram_in("ln1", [2, 128, 8])
    ln2_d = dram_in("ln2", [2, 128, 8])
    win_d = dram_in("w_in", [2, D, 8 * D])
    qg_d = dram_in("qg", [2, 128, 1])
    kg_d = dram_in("kg", [2, 128, 1])
    cw_d = dram_in("cw", [2, 128, 24])
    wa_d = dram_in("w_a", [2, D, D])
    wb_d = dram_in("w_b", [2, D, D])
    wo_d = dram_in("w_o", [2, D, D])
    wg_d = dram_in("w_g", [2, D, FF])
    wu_d = dram_in("w_u", [2, D, FF])
    wd_d = dram_in("w_d", [2, FF, D])
    maskA_d = dram_in("maskA", [128, 128], BF16)
    maskB_d = dram_in("maskB", [128, 128], BF16)
    ident_d = dram_in("ident", [128, 128], BF16)
    ntri_d = dram_in("ntri", [128, 128], BF16)
    nones_d = dram_in("nones", [128, 128], BF16)
    blk_d = dram_in("blk", [128, 128], BF16)
    identf_d = dram_in("identf", [128, 128])
    onesf_d = dram_in("onesf", [128, 128])
    out_d = nc.dram_tensor("out", [TOK, D], F32, kind="ExternalOutput")

    kt_own = [nc.dram_tensor(f"kt_own{g}", [512, TOK], BF16) for g in range(2)]
    kt_all = [nc.dram_tensor(f"kt_all{g}", [1024, TOK], BF16) for g in range(2)]
    v_own = [nc.dram_tensor(f"v_own{g}", [1024, D], BF16) for g in range(2)]
    v_all = [nc.dram_tensor(f"v_all{g}", [2048, D], BF16) for g in range(2)]
    qt_s = nc.dram_tensor("qt_s", [8, 128, TOK], BF16)
    ya_s = nc.dram_tensor("ya_s", [8, 128, TOK], BF16)
    yb_s = nc.dram_tensor("yb_s", [8, 128, TOK], BF16)
    wc_s = [nc.dram_tensor(f"wc_s{l}", [8, 128, 4, 8, 128], BF16) for l in range(2)]
    wf_s = [nc.dram_tensor(f"wf_s{l}", [NM, 128, 2, 8, 128], BF16) for l in range(2)]
    xh_own = nc.dram_tensor("xh_own", [32, D], F32)
    xh_all = nc.dram_tensor("xh_all", [64, D], F32)
    B_kt_own = [Buf(f"kt_own{g}") for g in range(2)]
    B_kt_all = [Buf(f"kt_all{g}") for g in range(2)]
    B_v_own = [Buf(f"v_own{g}") for g in range(2)]
    B_v_all = [Buf(f"v_all{g}") for g in range(2)]
    B_qt = [Buf(f"qt{i}") for i in range(8)]
    B_ya = [Buf(f"ya{i}") for i in range(8)]
    B_yb = [Buf(f"yb{i}") for i in range(8)]
    B_wc = [Buf(f"wc{l}") for l in range(2)]
    B_wf = [Buf(f"wf{l}") for l in range(2)]
    B_xho = Buf("xh_own")
    B_xha = Buf("xh_all")
    B_out = Buf("out")
    cring = [Buf("cr0"), Buf("cr1"), Buf("cr2")]
    cri = [0]

    arena = nc.alloc_sbuf_tensor("arena", [128, ARENA_BYTES // 2], BF16)

    def carve(off, shape, dt, name):
        assert off % 4 == 0
        n = 1
        for s_ in shape[1:]:
            n *= s_
        nb = n * (2 if dt == BF16 else 4)
        assert off + nb <= ARENA_BYTES, (name, off, nb)
        ap = arena[:, off // 2:(off + nb) // 2]
        if dt != BF16:
            ap = ap.bitcast(dt)
        if len(shape) == 3:
            ap = ap.rearrange("p (a b) -> p a b", a=shape[1])
        elif len(shape) == 4:
            ap = ap.rearrange("p (a b c) -> p a b c", a=shape[1], b=shape[2])
        if shape[0] < 128:
            ap = ap[0:shape[0]]
        return ap, nb

    class Alloc:
        def __init__(self, base):
            self.base = base
            self.p = base

        def new(self, name, shape, dt):
            ap, nb = carve(self.p, shape, dt, name)
            self.p += (nb + 31) // 32 * 32
            return Tile(ap, Buf(name))

        def reset(self):
            self.p = self.base

    fx = Alloc(0)
    x_sb = fx.new("x", [128, NSLOT, D], F32)
    Bx = [Buf(f"x{j}") for j in range(NSLOT)]
    hT = fx.new("hT", [128, 8, TOK + 32], BF16)
    assert fx.p <= 98816 + 64
    fx.p = 98816
    ident = fx.new("ident", [128, 128], BF16)
    ntri = fx.new("ntri", [128, 128], BF16)
    nones = fx.new("nones", [128, 128], BF16)
    blk = fx.new("blk", [128, 128], BF16)
    maskA = fx.new("maskA", [128, 128], BF16)
    maskB = fx.new("maskB", [128, 128], BF16)
    identf = fx.new("identf", [128, 128], F32)
    onesf = fx.new("onesf", [128, 128], F32)
    modc = [fx.new(f"modc{l}", [128, 48], F32) for l in range(2)]
    adab = [fx.new(f"adab{l}", [128, 48], F32) for l in range(2)]
    ln1 = [fx.new(f"ln1{l}", [128, 8], F32) for l in range(2)]
    ln2 = [fx.new(f"ln2{l}", [128, 8], F32) for l in range(2)]
    Amod = [[fx.new(f"A{l}{k}", [128, 8], F32) for k in range(2)] for l in range(2)]
    qg = [fx.new(f"qg{l}", [128, 1], F32) for l in range(2)]
    kg = [fx.new(f"kg{l}", [128, 1], F32) for l in range(2)]
    cw = [fx.new(f"cw{l}", [128, 24], F32) for l in range(2)]
    hmask = fx.new("hmask", [128, 32], F32)
    hsel = fx.new("hsel", [32, 2], F32)
    ccol = fx.new("ccol", [128, 8], F32)
    cact = fx.new("cact", [128, 8], BF16)
    stat = [fx.new(f"stat{i}", [128, 4], F32) for i in range(4)]
    assert fx.p <= WORK_BASE, fx.p
    wk = Alloc(WORK_BASE)

    pall = nc.alloc_psum_tensor("pall", [128, 4096], F32).ap()
    ps = [Tile(pall[:, i * 512:(i + 1) * 512], Buf(f"ps{i}")) for i in range(6)]
    psb = [Tile(pall[:, (6 + i) * 512:(7 + i) * 512].bitcast(BF16), Buf(f"psb{i}")) for i in range(2)]
    psctr = [0]

    def nextps():
        t = ps[psctr[0] % 6]
        psctr[0] += 1
        return t

    def ld(q, dst, src_ap, nobar=False):
        return S_.dma(q, [(dst.ap if isinstance(dst, Tile) else dst, src_ap)], dst, w=[dst], nobar=nobar)

    for t_, d_ in ((ident, ident_d), (ntri, ntri_d), (nones, nones_d), (blk, blk_d), (maskA, maskA_d),
                   (maskB, maskB_d), (identf, identf_d), (onesf, onesf_d), (hmask, hmask_d), (hsel, hsel_d),
                   (ccol, ccol_d)):
        S_.dma("sp", [(t_.ap, d_.ap())], t_, w=[t_], fixed=True)
    for l in range(2):
        for t_, d_ in ((adab[l], adab_d), (ln1[l], ln1_d), (ln2[l], ln2_d), (qg[l], qg_d), (kg[l], kg_d),
                       (cw[l], cw_d)):
            S_.dma("sp", [(t_.ap, d_.ap()[l])], t_, w=[t_], fixed=True)
    xv = x_d.ap().rearrange("(j p) f -> p j f", p=128)
    for q4 in range(4):
        S_.dma("sp", [(x_sb.ap[:, q4 * 4:(q4 + 1) * 4, :], xv[:, q4 * 4:(q4 + 1) * 4, :])], Bx[q4 * 4],
               w=[Bx[j] for j in range(q4 * 4, q4 * 4 + 4)], fixed=True)

    S_.op("act", lambda e: e.activation(out=cact.ap, in_=ccol.ap, func=AF.Silu), r=[ccol], w=[cact])
    def p0(l, wsec, pm):
        si = 0
        for sec in range(12):
            wt = wsec[si % 3]
            si += 1
            ld("pool", wt, adaw_d.ap()[l][:, sec * 512:(sec + 1) * 512].rearrange("(c p) n -> p c n", p=128))

            def fn(e, wt=wt, pm=pm, sec=sec):
                ins = None
                for fc in range(4):
                    c = sec * 4 + fc
                    for kc in range(8):
                        ins = e.matmul(pm.ap[:, c:c + 1], wt.ap[:, kc, fc * 128:(fc + 1) * 128], cact.ap[:, kc:kc + 1],
                                       start=(kc == 0), stop=(kc == 7), skip_group_check=True)
                return ins
            S_.op("pe", fn, r=[wt, cact], w=[pm])
            yield
        S_.op("dve", lambda e, l=l, pm=pm: e.tensor_tensor(out=modc[l].ap, in0=pm.ap[:, 0:48], in1=adab[l].ap, op=ALU.add),
              r=[pm, adab[l]], w=[modc[l]])
        for k, (c0, lnt) in enumerate(((8, ln1[l]), (32, ln2[l]))):
            S_.op("dve", lambda e, l=l, k=k, c0=c0, lnt=lnt: e.scalar_tensor_tensor(
                out=Amod[l][k].ap, in0=modc[l].ap[:, c0:c0 + 8], scalar=1.0, in1=lnt.ap, op0=ALU.add, op1=ALU.mult),
                r=[modc[l], lnt], w=[Amod[l][k]])
        yield

    for _ in p0(layers[0], [wk.new(f"wsec{i}", [128, 8, 512], BF16) for i in range(3)], nextps()):
        pass

    def norm_block(l, which, src_ap, src_buf, npart, dst_T, dst_cols, tmp):
        sq, xn, st, pb = tmp
        A = Amod[l][which]
        Bc0 = 0 if which == 0 else 24
        S_.op("act", lambda e: e.activation(out=sq.ap[0:npart], in_=src_ap, func=AF.Square), r=[src_buf], w=[sq])
        S_.op("dve", lambda e: e.tensor_reduce(out=st.ap[0:npart, 0:1], in_=sq.ap[0:npart], axis=AX.X, op=ALU.add),
              r=[sq], w=[st])
        S_.op("act", lambda e: e.activation(out=st.ap[0:npart, 1:2], in_=st.ap[0:npart, 0:1], func=AF.Ln,
                                            scale=1.0 / D, bias=EPS), r=[st], w=[st])
        S_.op("act", lambda e: e.activation(out=st.ap[0:npart, 2:3], in_=st.ap[0:npart, 1:2], func=AF.Exp, scale=-0.5),
              r=[st], w=[st])
        S_.op("dve", lambda e: e.tensor_scalar(out=xn.ap[0:npart], in0=src_ap, scalar1=st.ap[0:npart, 2:3], scalar2=None,
                                               op0=ALU.mult), r=[src_buf, st], w=[xn])

        def tr(e):
            ins = None
            for c in range(8):
                ins = e.transpose(pb.ap[:, c * 128:c * 128 + npart], xn.ap[0:npart, c * 128:(c + 1) * 128],
                                  ident.ap[0:npart, 0:npart])
            return ins
        S_.op("pe", tr, r=[xn, ident], w=[pb])

        def ev(e):
            ins = None
            for c in range(8):
                ins = e.tensor_scalar(out=dst_T.ap[:, c, dst_cols], in0=pb.ap[:, c * 128:c * 128 + npart],
                                      scalar1=A.ap[:, c:c + 1], scalar2=modc[l].ap[:, Bc0 + c:Bc0 + c + 1],
                                      op0=ALU.mult, op1=ALU.add)
            return ins
        S_.op("dve", ev, r=[pb, A, modc[l]], pw=[dst_T])

    def gbcast(l, c0, dst):
        for half in range(2):
            pg = nextps()
            dg = [wk.new(f"dg{half}{i}", [128, 128], F32) for i in range(4)]
            for c4 in range(4):
                c = c0 + half * 4 + c4
                S_.op("dve", lambda e, c=c, t=dg[c4]: e.tensor_scalar(out=t.ap, in0=identf.ap, scalar1=modc[l].ap[:, c:c + 1],
                                                                    scalar2=None, op0=ALU.mult), r=[identf, modc[l]], w=[dg[c4]])

            def fn(e, pg=pg, dg=dg):
                ins = None
                for c4 in range(4):
                    ins = e.matmul(pg.ap[:, c4 * 128:(c4 + 1) * 128], onesf.ap, dg[c4].ap, start=True, stop=True,
                                   skip_group_check=True)
                return ins
            S_.op("pe", fn, r=dg + [onesf], w=[pg])
            S_.op("act", lambda e, pg=pg, half=half: e.activation(out=dst.ap[:, half * 512:(half + 1) * 512], in_=pg.ap,
                                                                 func=AF.Identity), r=[pg], pw=[dst])

    def conv_gen():
        for l2 in layers:
            for m in range(8):
                prs = []
                for w_, src in enumerate((win_d.ap()[l2][:, 6 * D + m * 128:6 * D + (m + 1) * 128],
                                          win_d.ap()[l2][:, 7 * D + m * 128:7 * D + (m + 1) * 128],
                                          wa_d.ap()[l2][:, m * 128:(m + 1) * 128],
                                          wb_d.ap()[l2][:, m * 128:(m + 1) * 128])):
                    prs.append((wc_s[l2].ap()[m, :, w_, :, :], src.rearrange("(c p) n -> p c n", p=128)))
                S_.dma("pool", prs, cring[cri[0] % 3], pw=[B_wc[l2]], w=[cring[cri[0] % 3]], nobar=True, fixed=True)
                cri[0] += 1
                yield
            for m in range(NM):
                prs = []
                for w_, src in enumerate((wg_d.ap()[l2][:, m * 128:(m + 1) * 128], wu_d.ap()[l2][:, m * 128:(m + 1) * 128])):
                    prs.append((wf_s[l2].ap()[m, :, w_, :, :], src.rearrange("(c p) n -> p c n", p=128)))
                S_.dma("pool", prs, cring[cri[0] % 3], pw=[B_wf[l2]], w=[cring[cri[0] % 3]], nobar=True, fixed=True)
                cri[0] += 1
                yield


    def layer_body(li, l):
        last_layer = (li == len(layers) - 1)
        if stop == (li, 'P0'):
            return True
        S_.barrier()
        wk.reset()
        xh = wk.new("xh", [32, D], F32)
        if li == 0:
            ld("sp", xh, xh_d.ap())
        else:
            c0t = wk.new("c0t", [32, D], F32)
            c1t = wk.new("c1t", [32, D], F32)
            S_.dma("sp", [(xh_own.ap().rearrange("(j i) f -> i j f", i=2), x_sb.ap[126:128, :, :])], B_xho, r=Bx, w=[B_xho],
                   fixed=True)
            S_.collective(xh_own, xh_all, r=[B_xho], w=[B_xha])
            S_.dma("sp", [(c0t.ap, xh_all.ap()[0:32, :])], c0t, r=[B_xha], w=[c0t])
            S_.dma("sp", [(c1t.ap[2:32], xh_all.ap()[32:62, :]), (c1t.ap[0:2], xh_all.ap()[32:34, :])], c1t, r=[B_xha], w=[c1t])
            S_.op("dve", lambda e: e.tensor_scalar(out=c0t.ap, in0=c0t.ap, scalar1=hsel.ap[:, 0:1], scalar2=None, op0=ALU.mult),
                  r=[c0t, hsel], w=[c0t])
            S_.op("dve", lambda e: e.scalar_tensor_tensor(out=xh.ap, in0=c1t.ap, scalar=hsel.ap[:, 1:2], in1=c0t.ap,
                                                          op0=ALU.mult, op1=ALU.add), r=[c0t, c1t, hsel], w=[xh])
        tmps = [(wk.new(f"sq{i}", [128, D], BF16), wk.new(f"xn{i}", [128, D], BF16), stat[i], psb[i]) for i in range(2)]
        for j in range(NSLOT):
            norm_block(l, 0, x_sb.ap[:, j, :], Bx[j], 128, hT, slice(j * 128, (j + 1) * 128), tmps[j % 2])
        norm_block(l, 0, xh.ap, xh.buf, 32, hT, slice(TOK, TOK + 32), tmps[0])

        if stop == (li, 'A1'):
            return True
        S_.barrier()
        wk.reset()
        wsec = [wk.new(f"wsec{i}", [128, 8, 512], BF16) for i in range(3)]
        sqt = [wk.new(f"sqt{i}", [128, 512], BF16) for i in range(2)]
        lnv = [wk.new(f"lnv{i}", [128, 512], F32) for i in range(2)]
        rst = [wk.new(f"rst{i}", [128, 512], F32) for i in range(2)]
        ktile = [wk.new(f"ktile{i}", [128, TOK], BF16) for i in range(2)]
        vt = [wk.new(f"vt{i}", [128, 512], BF16) for i in range(3)]
        si = 0
        cnt = 0
        kti = 0
        vti = 0
        pending = [None]
        for kind, base in (("k", D), ("v", 2 * D), ("q", 0)):
            for s in range(2):
                wt = wsec[si % 3]
                si += 1
                ld("pool", wt, win_d.ap()[l][:, base + s * 512: base + (s + 1) * 512].rearrange("(c p) n -> p c n", p=128))
                if kind in ("k", "q"):
                    gcol = kg[l] if kind == "k" else qg[l]
                    ebias = 0.0 if kind == "k" else math.log(0.125)
                    for mm in range(4):
                        m = s * 4 + mm
                        kt_ = ktile[kti % 2]
                        kti += 1
                        for tc in range(4):
                            p1 = nextps()
                            i2 = cnt % 2
                            cnt += 1

                            def f1(e, p1=p1, wt=wt, mm=mm, tc=tc):
                                ins = None
                                for kc in range(8):
                                    ins = e.matmul(p1.ap, wt.ap[:, kc, mm * 128:(mm + 1) * 128], hT.ap[:, kc, tc * 512:(tc + 1) * 512],
                                                   start=(kc == 0), stop=(kc == 7))
                                return ins
                            S_.op("pe", f1, r=[wt, hT], w=[p1])
                            S_.op("act", lambda e, p1=p1, i2=i2: e.activation(out=sqt[i2].ap, in_=p1.ap, func=AF.Square),
                                  r=[p1], w=[sqt[i2]])
                            if pending[0] is not None:
                                pending[0]()

                            def back(p1=p1, i2=i2, kt_=kt_, tc=tc, gcol=gcol, ebias=ebias, kind=kind, s=s, mm=mm, m=m):
                                p2 = nextps()
                                S_.op("pe", lambda e: e.matmul(p2.ap, blk.ap, sqt[i2].ap, start=True, stop=True),
                                      r=[blk, sqt[i2]], w=[p2])
                                S_.op("act", lambda e: e.activation(out=lnv[i2].ap, in_=p2.ap, func=AF.Ln, scale=1.0 / 64, bias=EPS),
                                      r=[p2], w=[lnv[i2]])
                                S_.op("act", lambda e: e.activation(out=rst[i2].ap, in_=lnv[i2].ap, func=AF.Exp, scale=-0.5, bias=ebias),
                                      r=[lnv[i2]], w=[rst[i2]])
                                S_.op("dve", lambda e: e.scalar_tensor_tensor(
                                    out=kt_.ap[:, tc * 512:(tc + 1) * 512], in0=p1.ap, scalar=gcol.ap[:, 0:1], in1=rst[i2].ap,
                                    op0=ALU.mult, op1=ALU.mult), r=[p1, rst[i2], gcol], pw=[kt_])
                                if tc == 3:
                                    if kind == "k":
                                        S_.dma("sp", [(kt_own[s].ap()[mm * 128:(mm + 1) * 128, :], kt_.ap)], kt_, r=[kt_], pw=[B_kt_own[s]])
                                    else:
                                        S_.dma("sp", [(qt_s.ap()[m], kt_.ap)], kt_, r=[kt_], w=[B_qt[m]])
                            pending[0] = back
                    if pending[0] is not None:
                        pending[0]()
                        pending[0] = None
                    if kind == "k":
                        S_.collective(kt_own[s], kt_all[s], r=[B_kt_own[s]], w=[B_kt_all[s]])
                else:
                    for tb in range(NSLOT):
                        p1 = nextps()
                        v_ = vt[vti % 3]
                        vti += 1

                        def f1(e, p1=p1, wt=wt, tb=tb):
                            ins = None
                            for kc in range(8):
                                ins = e.matmul(p1.ap, hT.ap[:, kc, tb * 128:(tb + 1) * 128], wt.ap[:, kc, :],
                                               start=(kc == 0), stop=(kc == 7))
                            return ins
                        S_.op("pe", f1, r=[wt, hT], w=[p1])
                        S_.op("act", lambda e, p1=p1, v_=v_: e.activation(out=v_.ap, in_=p1.ap, func=AF.Identity), r=[p1], w=[v_])
                        g = tb // 8
                        S_.dma("sp", [(v_own[g].ap()[(tb % 8) * 128:(tb % 8 + 1) * 128, s * 512:(s + 1) * 512], v_.ap)], v_,
                               r=[v_], pw=[B_v_own[g]])
                    if s == 1:
                        for g in range(2):
                            S_.collective(v_own[g], v_all[g], r=[B_v_own[g]], w=[B_v_all[g]])

        if stop == (li, 'A2a'):
            return True
        S_.barrier()
        wk.reset()
        wconv = [wk.new(f"wconv{i}", [128, 8, 3, 512], BF16) for i in range(2)]
        ccs = [wk.new(f"ccs{i}", [128, 512], F32) for i in range(2)]
        ut = wk.new("ut", [128, TOK + 32], F32)
        cv = wk.new("cv", [128, TOK], F32)
        ybt = [wk.new(f"ybt{i}", [128, TOK], BF16) for i in range(2)]
        for s in range(2):
            S_.dma("pool", [(wconv[s].ap[:, :, j, :],
                             win_d.ap()[l][:, (3 + j) * D + s * 512:(3 + j) * D + (s + 1) * 512].rearrange("(c p) n -> p c n", p=128))
                            for j in range(3)], wconv[s], w=[wconv[s]])
        cwl = cw[l]
        for m in range(8):
            s = m // 4
            mm = m % 4
            wv = wconv[s]
            for tc in range(5):
                cols = slice(tc * 512, (tc + 1) * 512) if tc < 4 else slice(TOK, TOK + 32)
                n = 512 if tc < 4 else 32
                pc = nextps()
                px = nextps()
                cc_ = ccs[tc % 2]

                def fcc(e, pc=pc, wv=wv, mm=mm, cols=cols, n=n, j=1):
                    ins = None
                    for kc in range(8):
                        ins = e.matmul(pc.ap[:, 0:n], wv.ap[:, kc, j, mm * 128:(mm + 1) * 128], hT.ap[:, kc, cols],
                                       start=(kc == 0), stop=(kc == 7))
                    return ins
                S_.op("pe", fcc, r=[wv, hT], w=[pc])
                S_.op("act", lambda e, pc=pc, cc_=cc_, n=n: e.activation(out=cc_.ap[:, 0:n], in_=pc.ap[:, 0:n], func=AF.Identity),
                      r=[pc], w=[cc_])
                S_.op("pe", lambda e, px=px, wv=wv, mm=mm, cols=cols, n=n: fcc(e, px, wv, mm, cols, n, 2), r=[wv, hT], w=[px])
                S_.op("dve", lambda e, px=px, cc_=cc_, cols=cols, n=n: e.tensor_tensor(out=ut.ap[:, cols], in0=cc_.ap[:, 0:n],
                                                                                     in1=px.ap[:, 0:n], op=ALU.mult),
                      r=[px, cc_], pw=[ut])
            S_.op("dve", lambda e: e.tensor_tensor(out=ut.ap[:, TOK:TOK + 32], in0=ut.ap[:, TOK:TOK + 32], in1=hmask.ap, op=ALU.mult),
                  r=[ut, hmask], w=[ut])
            w0 = cwl.ap[:, 0 * 8 + m:0 * 8 + m + 1]
            w1 = cwl.ap[:, 1 * 8 + m:1 * 8 + m + 1]
            w2 = cwl.ap[:, 2 * 8 + m:2 * 8 + m + 1]
            u3 = ut.ap[:, 0:TOK].rearrange("p (j t) -> p j t", t=128)
            c3 = cv.ap.rearrange("p (j t) -> p j t", t=128)
            uh = ut.ap[:, TOK:TOK + 32].rearrange("p (j i) -> p j i", i=2)
            S_.op("dve", lambda e, w2=w2: e.tensor_scalar(out=cv.ap, in0=ut.ap[:, 0:TOK], scalar1=w2, scalar2=None, op0=ALU.mult),
                  r=[ut, cwl], w=[cv])
            S_.op("dve", lambda e, w1=w1, u3=u3, c3=c3: e.scalar_tensor_tensor(out=c3[:, :, 1:128], in0=u3[:, :, 0:127], scalar=w1,
                                                                            in1=c3[:, :, 1:128], op0=ALU.mult, op1=ALU.add),
                  r=[ut, cv, cwl], w=[cv])
            S_.op("dve", lambda e, w0=w0, u3=u3, c3=c3: e.scalar_tensor_tensor(out=c3[:, :, 2:128], in0=u3[:, :, 0:126], scalar=w0,
                                                                            in1=c3[:, :, 2:128], op0=ALU.mult, op1=ALU.add),
                  r=[ut, cv, cwl], w=[cv])
            for (ci, hi, wsc) in ((0, 1, w1), (0, 0, w0), (1, 1, w0)):
                S_.op("dve", lambda e, ci=ci, hi=hi, wsc=wsc, uh=uh, c3=c3: e.scalar_tensor_tensor(
                    out=c3[:, :, ci], in0=uh[:, :, hi], scalar=wsc, in1=c3[:, :, ci], op0=ALU.mult, op1=ALU.add),
                    r=[ut, cv, cwl], w=[cv])
            yb_ = ybt[m % 2]
            for tc in range(4):
                pb_ = nextps()
                cols = slice(tc * 512, (tc + 1) * 512)
                S_.op("pe", lambda e, pb_=pb_, wv=wv, mm=mm, cols=cols: fcc(e, pb_, wv, mm, cols, 512, 0), r=[wv, hT], w=[pb_])
                S_.op("dve", lambda e, pb_=pb_, yb_=yb_, cols=cols: e.tensor_tensor(out=yb_.ap[:, cols], in0=cv.ap[:, cols],
                                                                                  in1=pb_.ap, op=ALU.mult), r=[pb_, cv], pw=[yb_])
            S_.dma("sp", [(yb_s.ap()[m], yb_.ap)], yb_, r=[yb_], w=[B_yb[m]])

        if stop == (li, 'A2b'):
            return True
        S_.barrier()
        wk.reset()
        KT = [wk.new(f"KT{i}", [128, S], BF16) for i in range(2)]
        Vt = [wk.new(f"V{i}", [128, 32, 128], BF16) for i in range(2)]
        QT = [wk.new(f"QT{i}", [128, TOK], BF16) for i in range(2)]
        NR = 3
        et = [wk.new(f"e{i}", [128, 2, 512], F32) for i in range(NR)]
        spt = [wk.new(f"sp{i}", [128, 2, 512], BF16) for i in range(NR)]
        at = [wk.new(f"a{i}", [128, 2, 512], BF16) for i in range(NR)]
        sacc = wk.new("sacc", [128, 2, 512], BF16)
        yat = [wk.new(f"yat{i}", [128, TOK], BF16) for i in range(2)]
        zb = [Tile(pall[:, k * 1024:(k + 1) * 1024].rearrange("p (h n) -> p h n", h=2), Buf(f"z2_{k}")) for k in range(3)]
        OT = Tile(pall[:, 3072:3584], Buf("OT"))

        def load_hp(hp):
            g = hp // 4
            hl = hp % 4
            kt_, v_, q_ = KT[hp % 2], Vt[hp % 2], QT[hp % 2]
            S_.dma("sp", [(kt_.ap[:, r * TOK:(r + 1) * TOK], kt_all[g].ap()[r * 512 + hl * 128:r * 512 + (hl + 1) * 128, :])
                          for r in range(2)], kt_, r=[B_kt_all[g]], w=[kt_])
            S_.dma("sp", [(v_.ap[:, r * 16 + g2 * 8:r * 16 + g2 * 8 + 8, :],
                           v_all[g2].ap()[r * 1024:(r + 1) * 1024, hp * 128:(hp + 1) * 128].rearrange("(j p) c -> p j c", p=128))
                          for g2 in range(2) for r in range(2)], v_, r=B_v_all, w=[v_])
            S_.dma("sp", [(q_.ap, qt_s.ap()[hp])], q_, r=[B_qt[hp]], w=[q_])

        load_hp(0)
        items = []
        for hp in range(8):
            for G in range(4):
                for kb in range(8 * G + 7, -1, -1):
                    items.append((hp, G, kb))
        D1, D2 = 1, 2
        nit = len(items)

        def stage1(n, part):
            hp, G, kb = items[n]
            if part == 0 and G == 0 and kb == 7 and hp + 1 < 8:
                load_hp(hp + 1)
            i0 = (kb - 8 * G) // 2 if kb >= 8 * G else 0
            c0 = i0 * 128
            kt_, q_ = KT[hp % 2], QT[hp % 2]
            z = zb[n % 3]
            e_, sp_ = et[n % NR], spt[n % NR]
            kcol = ((kb % 2) * 16 + kb // 2) * 128

            def f1(e):
                ins = None
                for hd in range(2):
                    r0 = hd * 64
                    ins = e.matmul(z.ap[:, hd, c0:512], kt_.ap[r0:r0 + 64, kcol:kcol + 128],
                                   q_.ap[r0:r0 + 64, G * 512 + c0:(G + 1) * 512], start=True, stop=True)
                return ins
            if part == 0:
                S_.op("pe", f1, r=[kt_, q_], w=[z])
                S_.op("act", lambda e: e.activation(out=e_.ap[:, :, c0:512], in_=z.ap[:, :, c0:512], func=AF.Exp), r=[z], w=[e_])
                return
            S_.op("act", lambda e: e.activation(out=sp_.ap[:, :, c0:512], in_=e_.ap[:, :, c0:512], func=AF.Ln, bias=1.0),
                  r=[e_], w=[sp_])
            if kb >= 8 * G:
                mk = maskB if (kb - 8 * G) % 2 == 1 else maskA

                def fmk(e):
                    ins = None
                    for hd in range(2):
                        ins = e.tensor_tensor(out=sp_.ap[:, hd, c0:c0 + 128], in0=sp_.ap[:, hd, c0:c0 + 128], in1=mk.ap, op=ALU.mult)
                    return ins
                S_.op("dve", fmk, r=[sp_, mk], w=[sp_])

        def stage2(n):
            hp, G, kb = items[n]
            i0 = (kb - 8 * G) // 2 if kb >= 8 * G else 0
            c0 = i0 * 128
            z = zb[n % 3]
            sp_, a_ = spt[n % NR], at[n % NR]
            first = (kb == 8 * G + 7)
            if first:
                S_.op("dve", lambda e: e.memset(sacc.ap, 0.0), w=[sacc])

            def f2(e):
                ins = None
                for hd in range(2):
                    ins = e.matmul(z.ap[:, hd, c0:512], ntri.ap, sp_.ap[:, hd, c0:512], start=False, stop=True, skip_group_check=True)
                    if not first:
                        ins = e.matmul(z.ap[:, hd, c0:512], nones.ap, sacc.ap[:, hd, c0:512], start=False, stop=True,
                                       skip_group_check=True)
                return ins
            S_.op("pe", f2, r=[ntri, nones, sp_, sacc], w=[z])
            if kb > 0:
                S_.op("dve", lambda e: e.tensor_tensor(out=sacc.ap[:, :, c0:512], in0=sacc.ap[:, :, c0:512], in1=sp_.ap[:, :, c0:512],
                                                       op=ALU.add), r=[sp_, sacc], w=[sacc])
            S_.op("act", lambda e: e.activation(out=a_.ap[:, :, c0:512], in_=z.ap[:, :, c0:512], func=AF.Exp), r=[z], w=[a_])
            if kb >= 8 * G:
                mk = maskB if (kb - 8 * G) % 2 == 1 else maskA

                def fmk(e):
                    ins = None
                    for hd in range(2):
                        ins = e.tensor_tensor(out=a_.ap[:, hd, c0:c0 + 128], in0=a_.ap[:, hd, c0:c0 + 128], in1=mk.ap, op=ALU.mult)
                    return ins
                S_.op("dve", fmk, r=[a_, mk], w=[a_])

        def stage3(n):
            hp, G, kb = items[n]
            i0 = (kb - 8 * G) // 2 if kb >= 8 * G else 0
            c0 = i0 * 128
            a_ = at[n % NR]
            v_ = Vt[hp % 2]
            vb = (kb % 2) * 16 + kb // 2
            first = (kb == 8 * G + 7)

            def f3(e):
                ins = None
                for hd in range(2):
                    ins = e.matmul(OT.ap[hd * 64:(hd + 1) * 64, c0:512], v_.ap[:, vb, hd * 64:(hd + 1) * 64], a_.ap[:, hd, c0:512],
                                   start=first, stop=(kb == 0), skip_group_check=True)
                return ins
            S_.op("pe", f3, r=[v_, a_], w=[OT])
            if kb == 0:
                y_ = yat[hp % 2]
                S_.op("act", lambda e: e.activation(out=y_.ap[:, G * 512:(G + 1) * 512], in_=OT.ap, func=AF.Identity), r=[OT], pw=[y_])
                if G == 3:
                    S_.dma("sp", [(ya_s.ap()[hp], y_.ap)], y_, r=[y_], w=[B_ya[hp]])

        cgen = conv_gen() if li == 0 else None
        pgen = None
        if li == 0 and len(layers) > 1:
            pgen = p0(layers[1], [wk.new(f"wsecp{i}", [128, 8, 512], BF16) for i in range(3)],
                      Tile(pall[:, 3584:4096], Buf("pm7")))
        stage1(0, 0)
        for step in range(nit + D2):
            if pgen is not None and step % 10 == 2:
                if next(pgen, "done") == "done":
                    pgen = None
            elif pgen is None and cgen is not None and step % 3 == 0:
                if next(cgen, "done") == "done":
                    cgen = None
            if step < nit:
                stage1(step, 1)
            if step + 1 < nit:
                stage1(step + 1, 0)
            if 0 <= step - D1 < nit:
                stage2(step - D1)
            if 0 <= step - D2 < nit:
                stage3(step - D2)
        if cgen is not None:
            for _ in cgen:
                pass

        if stop == (li, 'ATT'):
            return True
        S_.barrier()
        wk.reset()
        wo = wk.new("wo", [128, 8, D], BF16)
        g1b = wk.new("g1b", [128, D], F32)
        wct = [wk.new(f"wct{i}", [128, 4, 8, 128], BF16) for i in range(2)]
        yac = wk.new("yac", [128, 8, 512], BF16)
        ybc = wk.new("ybc", [128, 8, 512], BF16)
        sgm = [wk.new(f"sgm{i}", [128, 512], F32) for i in range(4)]
        tt = [wk.new(f"tt{i}", [128, 512], F32) for i in range(4)]
        mg = wk.new("mg", [128, 8, 512], BF16)
        ld("pool", wo, wo_d.ap()[l].rearrange("(c p) n -> p c n", p=128))
        gbcast(l, 16, g1b)
        wi = 0
        for tc in range(4):
            cols = slice(tc * 512, (tc + 1) * 512)
            S_.dma("sp", [(yac.ap, ya_s.ap()[:, :, cols].rearrange("m p t -> p m t"))], yac, r=B_ya, w=[yac])
            S_.dma("sp", [(ybc.ap, yb_s.ap()[:, :, cols].rearrange("m p t -> p m t"))], ybc, r=B_yb, w=[ybc])
            for m in range(8):
                wc_ = wct[wi % 2]
                wi += 1
                S_.dma("sp", [(wc_.ap, wc_s[l].ap()[m])], wc_, r=[B_wc[l]], w=[wc_])
                pp = [nextps() for _ in range(4)]
                for w_ in range(4):
                    rhs_t = hT if w_ < 2 else (yac if w_ == 2 else ybc)

                    def fm(e, p=pp[w_], wc_=wc_, w_=w_, rhs_t=rhs_t, cols=cols):
                        ins = None
                        for kc in range(8):
                            rhs = rhs_t.ap[:, kc, cols] if w_ < 2 else rhs_t.ap[:, kc, :]
                            ins = e.matmul(p.ap, wc_.ap[:, w_, kc, :], rhs, start=(kc == 0), stop=(kc == 7))
                        return ins
                    S_.op("pe", fm, r=[wc_, rhs_t], w=[pp[w_]])
                sa, sb = sgm[(2 * m) % 4], sgm[(2 * m + 1) % 4]
                t1, t2 = tt[(2 * m) % 4], tt[(2 * m + 1) % 4]
                S_.op("act", lambda e, p=pp[0], sa=sa: e.activation(out=sa.ap, in_=p.ap, func=AF.Sigmoid), r=[pp[0]], w=[sa])
                S_.op("act", lambda e, p=pp[1], sb=sb: e.activation(out=sb.ap, in_=p.ap, func=AF.Sigmoid), r=[pp[1]], w=[sb])
                S_.op("dve", lambda e, p=pp[2], sa=sa, t1=t1: e.tensor_tensor(out=t1.ap, in0=sa.ap, in1=p.ap, op=ALU.mult),
                      r=[pp[2], sa], w=[t1])
                S_.op("dve", lambda e, p=pp[3], sb=sb, t2=t2: e.tensor_tensor(out=t2.ap, in0=sb.ap, in1=p.ap, op=ALU.mult),
                      r=[pp[3], sb], w=[t2])
                S_.op("dve", lambda e, t1=t1, t2=t2, m=m: e.tensor_tensor(out=mg.ap[:, m, :], in0=t1.ap, in1=t2.ap, op=ALU.add),
                      r=[t1, t2], pw=[mg])
            if tc == 0:
                for kc in range(8):
                    S_.op("dve", lambda e, kc=kc: e.tensor_tensor(out=wo.ap[:, kc, :], in0=wo.ap[:, kc, :], in1=g1b.ap, op=ALU.mult),
                          r=[wo, g1b], w=[wo])
            for tb in range(4):
                j = tc * 4 + tb
                for ch in range(2):
                    po = nextps()

                    def fo(e, po=po, tb=tb, ch=ch):
                        ins = None
                        for m in range(8):
                            ins = e.matmul(po.ap, mg.ap[:, m, tb * 128:(tb + 1) * 128], wo.ap[:, m, ch * 512:(ch + 1) * 512],
                                           start=(m == 0), stop=(m == 7))
                        return ins
                    S_.op("pe", fo, r=[mg, wo], w=[po])
                    S_.op("dve", lambda e, po=po, j=j, ch=ch: e.tensor_tensor(out=x_sb.ap[:, j, ch * 512:(ch + 1) * 512],
                                                                            in0=x_sb.ap[:, j, ch * 512:(ch + 1) * 512], in1=po.ap, op=ALU.add),
                          r=[po, Bx[j]], w=[Bx[j]])

        if stop == (li, 'C'):
            return True
        S_.barrier()
        wk.reset()
        wd = wk.new("wd", [128, NM, D], BF16)
        g2b = wk.new("g2b", [128, D], F32)
        wft = [wk.new(f"wft{i}", [128, 2, 8, 128], BF16) for i in range(2)]
        h2c = wk.new("h2c", [128, 8, 512], BF16)
        actT = wk.new("actT", [128, NM, 512], BF16)
        sg = [wk.new(f"sg{i}", [128, 512], F32) for i in range(2)]
        tmps = [(wk.new(f"sq{i}", [128, D], BF16), wk.new(f"xn{i}", [128, D], BF16), stat[i], psb[i]) for i in range(2)]
        wdv = wd_d.ap()[l].rearrange("(m p) n -> p m n", p=128)
        for q3 in range(2):
            S_.dma("pool", [(wd.ap[:, q3 * 11:(q3 + 1) * 11, :], wdv[:, q3 * 11:(q3 + 1) * 11, :])], wd, pw=[wd])
        gbcast(l, 40, g2b)
        wi = 0
        for tc in range(4):
            for tb in range(4):
                j = tc * 4 + tb
                norm_block(l, 1, x_sb.ap[:, j, :], Bx[j], 128, h2c, slice(tb * 128, (tb + 1) * 128), tmps[tb % 2])
            for m in range(NM):
                wf_ = wft[wi % 2]
                wi += 1
                S_.dma("sp", [(wf_.ap, wf_s[l].ap()[m])], wf_, r=[B_wf[l]], w=[wf_])
                pg, pu = nextps(), nextps()
                for w_, p in ((0, pg), (1, pu)):
                    def fm(e, p=p, wf_=wf_, w_=w_):
                        ins = None
                        for kc in range(8):
                            ins = e.matmul(p.ap, wf_.ap[:, w_, kc, :], h2c.ap[:, kc, :], start=(kc == 0), stop=(kc == 7))
                        return ins
                    S_.op("pe", fm, r=[wf_, h2c], w=[p])
                sg_ = sg[m % 2]
                S_.op("act", lambda e, pg=pg, sg_=sg_: e.activation(out=sg_.ap, in_=pg.ap, func=AF.Silu), r=[pg], w=[sg_])
                S_.op("dve", lambda e, pu=pu, sg_=sg_, m=m: e.tensor_tensor(out=actT.ap[:, m, :], in0=sg_.ap, in1=pu.ap, op=ALU.mult),
                      r=[pu, sg_], pw=[actT])
            if tc == 0:
                for m in range(NM):
                    S_.op("dve", lambda e, m=m: e.tensor_tensor(out=wd.ap[:, m, :], in0=wd.ap[:, m, :], in1=g2b.ap, op=ALU.mult),
                          r=[wd, g2b], w=[wd])
            for tb in range(4):
                j = tc * 4 + tb
                for ch in range(2):
                    po = nextps()

                    def fo(e, po=po, tb=tb, ch=ch):
                        ins = None
                        for m in range(NM):
                            ins = e.matmul(po.ap, actT.ap[:, m, tb * 128:(tb + 1) * 128], wd.ap[:, m, ch * 512:(ch + 1) * 512],
                                           start=(m == 0), stop=(m == NM - 1))
                        return ins
                    S_.op("pe", fo, r=[actT, wd], w=[po])
                    S_.op("dve", lambda e, po=po, j=j, ch=ch: e.tensor_tensor(out=x_sb.ap[:, j, ch * 512:(ch + 1) * 512],
                                                                            in0=x_sb.ap[:, j, ch * 512:(ch + 1) * 512], in1=po.ap, op=ALU.add),
                          r=[po, Bx[j]], w=[Bx[j]])
                if last_layer:
                    S_.dma("sp", [(out_d.ap()[j * 128:(j + 1) * 128, :], x_sb.ap[:, j, :])], Bx[j], r=[Bx[j]], pw=[B_out], fixed=True)

        return False

    for li, l in enumerate(layers):
        if layer_body(li, l):
            break

    S_.barrier()
    S_.op("sp", lambda e: None)
    S_.emit()
    return nc


_CACHE = {}


def _consts():
    bf = ml_dtypes.bfloat16
    i = np.arange(128)
    ident = np.eye(128, dtype=np.float32)
    ntri = -(i[:, None] >= i[None, :]).astype(np.float32)
    nones = -np.ones((128, 128), np.float32)
    blkm = (i[:, None] // 64 == i[None, :] // 64).astype(np.float32)
    causal = (i[:, None] < i[None, :]).astype(np.float32)
    return dict(ident=ident.astype(bf), ntri=ntri.astype(bf), nones=nones.astype(bf), blk=blkm.astype(bf),
                identf=ident, onesf=np.ones((128, 128), np.float32)), causal


def make_in_maps(x, c, ada_w, ada_b, ln1_g, w_in, q_norm_g, k_norm_g, conv_w, w_branch_a, w_branch_b, w_out,
                 ln2_g, w_ffn_gate, w_ffn_up, w_ffn_down):
    bf = ml_dtypes.bfloat16
    f = lambda a: np.ascontiguousarray(np.asarray(a, dtype=np.float32))
    x, c = f(x), f(c)
    consts, causal = _consts()
    shared = dict(
        ada_w=f(ada_w),
        ada_b=f(np.asarray(ada_b, np.float32).reshape(2, 48, 128).transpose(0, 2, 1)),
        ln1=f(np.asarray(ln1_g, np.float32).reshape(2, 8, 128).transpose(0, 2, 1)),
        ln2=f(np.asarray(ln2_g, np.float32).reshape(2, 8, 128).transpose(0, 2, 1)),
        w_in=f(w_in),
        qg=f(np.tile(np.asarray(q_norm_g, np.float32), (1, 2)).reshape(2, 128, 1)),
        kg=f(np.tile(np.asarray(k_norm_g, np.float32), (1, 2)).reshape(2, 128, 1)),
        cw=f(np.asarray(conv_w, np.float32).reshape(2, 3, 8, 128).transpose(0, 3, 1, 2).reshape(2, 128, 24)),
        w_a=f(w_branch_a), w_b=f(w_branch_b), w_o=f(w_out), w_g=f(w_ffn_gate), w_u=f(w_ffn_up), w_d=f(w_ffn_down),
        **consts)
    zeros = np.zeros((128, 128), np.float32)
    ones = np.ones((128, 128), np.float32)
    in_maps = []
    for core in range(8):
        b, h = core // 2, core % 2
        xb = x[b].reshape(32, 128, D)
        xo = np.ascontiguousarray(xb[h::2].reshape(TOK, D))
        xh = np.zeros((32, D), np.float32)
        for j in range(NSLOT):
            t0 = (2 * j + h) * 128
            if t0 >= 2:
                xh[2 * j:2 * j + 2] = x[b, t0 - 2:t0]
        hmask = np.ones((128, 32), np.float32)
        if h == 0:
            hmask[:, 0:2] = 0.0
        hsel = np.zeros((32, 2), np.float32)
        hsel[:, 0] = 1.0 if h == 1 else 0.0
        hsel[:, 1] = 0.0 if h == 1 else 1.0
        mA = causal if h == 0 else ones
        mB = zeros if h == 0 else causal
        m = dict(shared)
        m.update(x=xo, xh=xh, hmask=hmask, hsel=hsel,
                 ccol=np.ascontiguousarray(c[b].reshape(8, 128).T),
                 maskA=mA.astype(bf), maskB=mB.astype(bf))
        in_maps.append(m)
    return in_maps


def assemble(results):
    out = np.zeros((NB, S, D), np.float32)
    for core in range(8):
        b, h = core // 2, core % 2
        o = np.asarray(results[core]["out"], dtype=np.float32).reshape(NSLOT, 128, D)
        out[b].reshape(32, 128, D)[h::2] = o
    return out


def kernel(**inputs):
    if "nc" not in _CACHE:
        _CACHE["nc"] = build((0, 1))
    nc = _CACHE["nc"]
    in_maps = make_in_maps(**inputs)
    res = run_bass_kernel_spmd(nc, in_maps, core_ids=list(range(8)))
    return assemble(res.results)
```
